# Optimizing a Trainium2 kernel written in Bass

```python
import math
import jax, jax.numpy as jnp
from jax import lax
import numpy as np

D_MODEL = 2048
BATCH = 4
SEQ = 4096
DEPTH = 1

HEAD_DIM = 128
N_MEM = 256
GRID_W = 64
DIL_PAIRS = ((128, 1), (512, 4), (2048, 16))
A_HEADS_PER_GROUP = 2
A_HEADS = A_HEADS_PER_GROUP * len(DIL_PAIRS)
A_BLOCK = 64
B_Q_HEADS = 6
B_KV_HEADS = 2
B_QBLOCK = 128
ROPE_THETA = 10000.0
C_HEADS = 4
N_BRANCH = 3
REL_BUCKETS = 32
REL_MAX_DIST = 1024
D_FF = -(-8 * D_MODEL // (3 * 256)) * 256
EPS = 1e-6
NEG = -1e30

A_W = A_HEADS * HEAD_DIM
A_OUT_W = A_HEADS_PER_GROUP * HEAD_DIM
B_QW = B_Q_HEADS * HEAD_DIM
B_KVW = B_KV_HEADS * HEAD_DIM
C_W = C_HEADS * HEAD_DIM
IN_SPLITS = (A_W, A_W, A_W, B_QW, B_KVW, B_KVW, C_W, N_BRANCH * D_MODEL)
IN_W = sum(IN_SPLITS)

kernel_name = "hybrid_gated_dilated_gqa_mem_encoder"


def rms_norm(x, g):
    xf = x.astype(jnp.float32)
    y = xf * lax.rsqrt(jnp.mean(xf * xf, axis=-1, keepdims=True) + EPS)
    return (y * g.astype(jnp.float32)).astype(x.dtype)


def split_heads(t, h):
    return t.reshape(t.shape[0], t.shape[1], h, HEAD_DIM)


def t5_bucket(rel):
    nb = REL_BUCKETS // 2
    ret = jnp.where(rel > 0, nb, 0)
    n = jnp.abs(rel)
    max_exact = nb // 2
    large = max_exact + (jnp.log(jnp.maximum(n, 1).astype(jnp.float32) / max_exact)
                         / math.log(REL_MAX_DIST / max_exact) * (nb - max_exact)).astype(jnp.int32)
    large = jnp.minimum(large, nb - 1)
    return ret + jnp.where(n < max_exact, n, large)


def dilated_group(q, k, v, bias_tab, window, dil):
    b, s, h, dh = q.shape
    L = s // dil
    radius = window // (2 * dil)
    n = b * dil

    def to_res(t):
        return t.reshape(b, L, dil, h, dh).transpose(0, 2, 1, 3, 4).reshape(n, L, h, dh)

    qr, kr, vr = to_res(q), to_res(k), to_res(v)
    nb = -(-L // A_BLOCK)
    lp = nb * A_BLOCK
    side = -(-radius // A_BLOCK)
    pad = side * A_BLOCK
    wk = (2 * side + 1) * A_BLOCK
    qb = jnp.pad(qr, ((0, 0), (0, lp - L), (0, 0), (0, 0))).reshape(n, nb, A_BLOCK, h, dh)

    def band(t):
        tp = jnp.pad(t, ((0, 0), (pad, lp - L + pad), (0, 0), (0, 0))).reshape(n, nb + 2 * side, A_BLOCK, h, dh)
        return jnp.concatenate([tp[:, i:i + nb] for i in range(2 * side + 1)], axis=2)

    kb, vb = band(kr), band(vr)
    rel = (jnp.arange(wk) - pad)[None, :] - jnp.arange(A_BLOCK)[:, None]
    bias = bias_tab[t5_bucket(rel * dil)].transpose(2, 0, 1).astype(jnp.float32)
    k_pos = jnp.arange(nb)[:, None] * A_BLOCK - pad + jnp.arange(wk)[None, :]
    valid = (jnp.abs(rel) <= radius)[None] & ((k_pos >= 0) & (k_pos < L))[:, None, :]
    sc = jnp.einsum('nbqhd,nbkhd->nbhqk', qb, kb, preferred_element_type=jnp.float32) / math.sqrt(dh)
    sc = jnp.where(valid[None, :, None], sc + bias[None, None], NEG)
    m = jnp.max(sc, axis=-1, keepdims=True)
    p = jnp.exp(sc - m)
    l = jnp.sum(p, axis=-1, keepdims=True)
    o = jnp.einsum('nbhqk,nbkhd->nbqhd', p, vb.astype(jnp.float32))
    o = o / l[..., 0].transpose(0, 1, 3, 2)[..., None]
    lse = (m + jnp.log(l))[..., 0].transpose(0, 1, 3, 2)
    o = o.reshape(n, lp, h, dh)[:, :L]
    lse = lse.reshape(n, lp, h)[:, :L]
    o = o.reshape(b, dil, L, h, dh).transpose(0, 2, 1, 3, 4).reshape(b, s, h, dh)
    lse = lse.reshape(b, dil, L, h).transpose(0, 2, 1, 3).reshape(b, s, h)
    return o, lse


def axial_rope_tables(s):
    rows = s // GRID_W
    r = jnp.repeat(jnp.arange(rows), GRID_W).astype(jnp.float32)
    c = jnp.tile(jnp.arange(GRID_W), rows).astype(jnp.float32)
    nf = HEAD_DIM // 4
    inv = ROPE_THETA ** (-jnp.arange(nf, dtype=jnp.float32) / nf)
    ang = jnp.concatenate([r[:, None] * inv, c[:, None] * inv], axis=-1)
    return jnp.cos(ang), jnp.sin(ang)


def apply_rope(x, cos, sin):
    b, s, h, dh = x.shape
    xf = x.astype(jnp.float32).reshape(b, s, h, dh // 2, 2)
    x0, x1 = xf[..., 0], xf[..., 1]
    c, sn = cos[None, :, None, :], sin[None, :, None, :]
    out = jnp.stack([x0 * c - x1 * sn, x0 * sn + x1 * c], axis=-1)
    return out.reshape(b, s, h, dh).astype(x.dtype)


def gqa_blocks(q, k, v):
    b, s, hq, dh = q.shape
    g = hq // B_KV_HEADS
    nblk = s // B_QBLOCK
    qb = q.reshape(b, nblk, B_QBLOCK, B_KV_HEADS, g, dh).transpose(1, 0, 2, 3, 4, 5)

    def one_block(qq):
        sc = jnp.einsum('bqkgd,bskd->bkgqs', qq, k, preferred_element_type=jnp.float32) / math.sqrt(dh)
        p = jax.nn.softmax(sc, axis=-1)
        return jnp.einsum('bkgqs,bskd->bqkgd', p, v.astype(jnp.float32)).astype(q.dtype)

    o = lax.map(one_block, qb)
    return o.transpose(1, 0, 2, 3, 4, 5).reshape(b, s, hq * dh)


def setup_inputs(seed: int = 0) -> dict:
    key = jax.random.key(seed)
    ks = jax.random.split(key, 24)
    f32 = jnp.float32

    def w(k, shape, fan_in):
        return jax.random.normal(k, shape, f32) * (fan_in ** -0.5)

    def gain(k, shape):
        return 1.0 + 0.05 * jax.random.normal(k, shape, f32)

    Dp = DEPTH
    return {
        "x": jax.random.normal(ks[0], (BATCH, SEQ, D_MODEL), f32),
        "mem": jax.random.normal(ks[1], (BATCH, N_MEM, D_MODEL), f32),
        "rel_bias": 0.5 * jax.random.normal(ks[2], (REL_BUCKETS, A_HEADS), f32),
        "g_mix": gain(ks[3], (Dp, D_MODEL)),
        "w_in": w(ks[4], (Dp, D_MODEL, IN_W), D_MODEL),
        "g_qa": gain(ks[5], (Dp, HEAD_DIM)),
        "g_ka": gain(ks[6], (Dp, HEAD_DIM)),
        "g_qb": gain(ks[7], (Dp, HEAD_DIM)),
        "g_kb": gain(ks[8], (Dp, HEAD_DIM)),
        "g_mem": gain(ks[9], (Dp, D_MODEL)),
        "w_mem_kv": w(ks[10], (Dp, D_MODEL, 2 * C_W), D_MODEL),
        "g_qc": gain(ks[11], (Dp, HEAD_DIM)),
        "g_kc": gain(ks[12], (Dp, HEAD_DIM)),
        "w_br_a": w(ks[13], (Dp, A_OUT_W, D_MODEL), A_OUT_W),
        "w_br_b": w(ks[14], (Dp, B_QW, D_MODEL), B_QW),
        "w_br_c": w(ks[15], (Dp, C_W, D_MODEL), C_W),
        "w_o": w(ks[16], (Dp, D_MODEL, D_MODEL), D_MODEL),
        "g_ffn": gain(ks[17], (Dp, D_MODEL)),
        "w_ffn_in": w(ks[18], (Dp, D_MODEL, 2 * D_FF), D_MODEL),
        "w_ffn_out": w(ks[19], (Dp, D_FF, D_MODEL), D_FF),
    }


def reference(x, mem, rel_bias, g_mix, w_in, g_qa, g_ka, g_qb, g_kb, g_mem, w_mem_kv,
              g_qc, g_kc, w_br_a, w_br_b, w_br_c, w_o, g_ffn, w_ffn_in, w_ffn_out):
    b, s, _ = x.shape
    cos, sin = axial_rope_tables(s)
    offs = np.cumsum((0,) + IN_SPLITS)
    for layer in range(DEPTH):
        h = rms_norm(x, g_mix[layer])
        proj = h @ w_in[layer]
        qa, ka, va, qb, kb, vb, qc, gt = [proj[..., offs[i]:offs[i + 1]] for i in range(len(IN_SPLITS))]

        qa = rms_norm(split_heads(qa, A_HEADS), g_qa[layer])
        ka = rms_norm(split_heads(ka, A_HEADS), g_ka[layer])
        va = split_heads(va, A_HEADS)
        outs, lses = [], []
        for gi, (win, dil) in enumerate(DIL_PAIRS):
            sl = slice(gi * A_HEADS_PER_GROUP, (gi + 1) * A_HEADS_PER_GROUP)
            o, lse = dilated_group(qa[:, :, sl], ka[:, :, sl], va[:, :, sl], rel_bias[:, sl], win, dil)
            outs.append(o)
            lses.append(lse)
        wgt = jax.nn.softmax(jnp.stack(lses), axis=0)
        oa = jnp.sum(wgt[..., None] * jnp.stack(outs), axis=0).reshape(b, s, A_OUT_W).astype(x.dtype)

        qb = apply_rope(rms_norm(split_heads(qb, B_Q_HEADS), g_qb[layer]), cos, sin)
        kb = apply_rope(rms_norm(split_heads(kb, B_KV_HEADS), g_kb[layer]), cos, sin)
        ob = gqa_blocks(qb, kb, split_heads(vb, B_KV_HEADS))

        mkv = rms_norm(mem, g_mem[layer]) @ w_mem_kv[layer]
        kc = rms_norm(split_heads(mkv[..., :C_W], C_HEADS), g_kc[layer])
        vc = split_heads(mkv[..., C_W:], C_HEADS)
        qc = rms_norm(split_heads(qc, C_HEADS), g_qc[layer])
        sc = jnp.einsum('bqhd,bmhd->bhqm', qc, kc, preferred_element_type=jnp.float32) / math.sqrt(HEAD_DIM)
        pc = jax.nn.softmax(sc, axis=-1)
        oc = jnp.einsum('bhqm,bmhd->bqhd', pc, vc.astype(jnp.float32)).reshape(b, s, C_W).astype(x.dtype)

        gates = jax.nn.sigmoid(gt.astype(jnp.float32)).reshape(b, s, N_BRANCH, D_MODEL)
        merged = (gates[:, :, 0] * (oa @ w_br_a[layer]).astype(jnp.float32)
                  + gates[:, :, 1] * (ob @ w_br_b[layer]).astype(jnp.float32)
                  + gates[:, :, 2] * (oc @ w_br_c[layer]).astype(jnp.float32)).astype(x.dtype)
        x = x + merged @ w_o[layer]

        hf = rms_norm(x, g_ffn[layer]) @ w_ffn_in[layer]
        x = x + (jax.nn.silu(hf[..., :D_FF]) * hf[..., D_FF:]) @ w_ffn_out[layer]
    return x
```

```python
import contextlib
import math
import numpy as np
import concourse.bass as bass
import concourse.mybir as mybir
from concourse.bass_utils import run_bass_kernel_spmd

F32 = mybir.dt.float32
BF16 = mybir.dt.bfloat16
U8 = mybir.dt.uint8
AF = mybir.ActivationFunctionType
ALU = mybir.AluOpType
AX = mybir.AxisListType

D = 2048
SEQ = 4096
NOWN = 2048
DFF = 5632
EPS = 1e-6
SCALE = 1.0 / math.sqrt(128.0)
NEGB = -30000.0
DILS = (1, 4, 16)
ENG_NAMES = ("pe", "act", "dve", "pool", "sp")


class Op:
    __slots__ = ("eng", "fn", "reads", "writes", "dma", "deps", "need_inc", "cnt", "idx", "bar", "spool")

    def __init__(self, eng, fn, reads, writes, dma):
        self.eng = eng
        self.fn = fn
        self.reads = reads
        self.writes = writes
        self.dma = dma
        self.deps = ()
        self.need_inc = False
        self.cnt = 0
        self.bar = False
        self.spool = eng


class Rec:
    def __init__(self):
        self.ops = []

    disabled = False

    def op(self, eng, fn, reads=(), writes=(), dma=False):
        if self.disabled:
            return None
        o = Op(eng, fn, tuple(reads), tuple(writes), dma)
        o.idx = len(self.ops)
        self.ops.append(o)
        return o

    def pe(self, fn, reads=(), writes=()):
        return self.op("pe", fn, reads, writes)

    def act(self, fn, reads=(), writes=()):
        return self.op("act", fn, reads, writes)

    def dve(self, fn, reads=(), writes=()):
        return self.op("dve", fn, reads, writes)

    def pool(self, fn, reads=(), writes=()):
        return self.op("pool", fn, reads, writes)

    def dma(self, eng, fn, reads=(), writes=(), spool=None):
        o = self.op(eng, fn, reads, writes, dma=True)
        if o is not None and spool is not None:
            o.spool = spool
        return o

    def barrier(self):
        if self.disabled:
            return
        for e in ENG_NAMES:
            o = self.op(e, None)
            o.bar = True

    def analyze(self):
        last_w = {}
        readers = {}
        last_on = {}
        dmas_since = []
        i = 0
        n = len(self.ops)
        while i < n:
            o = self.ops[i]
            if o.bar:
                grp = []
                while i < n and self.ops[i].bar:
                    grp.append(self.ops[i])
                    i += 1
                forced = [v for v in last_on.values()] + list(dmas_since)
                for b in grp:
                    b.deps = tuple(sorted(forced))
                for j in forced:
                    self.ops[j].need_inc = True
                last_w = {}
                readers = {}
                dmas_since = []
                continue
            deps = set()
            for k in o.reads:
                w = last_w.get(k)
                if w is not None:
                    deps.add(w)
            for k in o.writes:
                w = last_w.get(k)
                if w is not None:
                    deps.add(w)
                for r in readers.get(k, ()):
                    deps.add(r)
            deps.discard(o.idx)
            keep = []
            for j in deps:
                p = self.ops[j]
                if (not p.dma) and (not o.dma) and p.eng == o.eng:
                    if o.eng == "pe":
                        continue
                keep.append(j)
            best = {}
            kept = []
            for j in keep:
                p = self.ops[j]
                if p.dma:
                    kept.append(j)
                elif j > best.get(p.eng, -1):
                    best[p.eng] = j
            o.deps = tuple(sorted(kept + list(best.values())))
            for j in o.deps:
                self.ops[j].need_inc = True
            for k in o.reads:
                readers.setdefault(k, []).append(o.idx)
            for k in o.writes:
                last_w[k] = o.idx
                readers[k] = []
            if o.dma:
                dmas_since.append(o.idx)
            else:
                last_on[o.eng] = o.idx
            i += 1

    def prepare(self, sems):
        self.analyze()
        ccount = {e: 0 for e in ENG_NAMES}
        dma_sems = sems["dma"]
        self.dma_cnt = {e: [0] * len(dma_sems[e]) for e in dma_sems}
        rr = {e: 0 for e in dma_sems}
        for o in self.ops:
            if o.bar:
                continue
            if o.dma:
                nse = len(dma_sems[o.spool])
                s = rr[o.spool] % nse
                rr[o.spool] += 1
                self.dma_cnt[o.spool][s] += 16
                o.cnt = (o.spool, s, self.dma_cnt[o.spool][s])
            elif o.need_inc:
                ccount[o.eng] += 1
                o.cnt = ccount[o.eng]
        self.streams = {e: [] for e in ENG_NAMES}
        for o in self.ops:
            self.streams[o.eng].append(o)
        return ccount

    def run_engine(self, ename, eng, sems):
        waited = {}
        dma_sems = sems["dma"]
        for o in self.streams[ename]:
            need = {}
            for j in o.deps:
                p = self.ops[j]
                if p.dma:
                    pe_, s, v = p.cnt
                    key = ("d", pe_, s)
                else:
                    key = p.eng
                    v = p.cnt
                if v > need.get(key, 0):
                    need[key] = v
            if o.dma:
                sp_, s, v = o.cnt
                key = ("d", sp_, s)
                if v > 16 and need.get(key, 0) < v - 16:
                    need[key] = v - 16
            for key, v in need.items():
                if waited.get(key, 0) >= v:
                    continue
                if isinstance(key, tuple):
                    eng.wait_ge(dma_sems[key[1]][key[2]], v)
                else:
                    eng.wait_ge(sems[key], v)
                waited[key] = v
            if o.bar:
                continue
            ins = o.fn(eng)
            if o.dma:
                sp_, s, v = o.cnt
                ins.then_inc(dma_sems[sp_][s], 16)
            elif o.need_inc:
                ins.then_inc(sems[ename], 1)

    def final_waits(self, ename, eng, sems):
        for pool_name, cnts in self.dma_cnt.items():
            if pool_name.split(":")[0] != ename:
                continue
            for s, v in enumerate(cnts):
                if v > 0:
                    eng.wait_ge(sems["dma"][pool_name][s], v)


class Rot:
    def __init__(self, name, aps, track=False):
        self.name = name
        self.aps = aps
        self.i = 0
        self.track = track
        self.held = [False] * len(aps)

    def next(self):
        n = len(self.aps)
        if not self.track:
            i = self.i % n
            self.i += 1
            return self.aps[i], (self.name, i)
        for d in range(n):
            i = (self.i + d) % n
            if not self.held[i]:
                self.held[i] = True
                self.i = i + 1
                return self.aps[i], (self.name, i)
        raise RuntimeError(f"rotation {self.name} exhausted ({n} buffers)")

    def release(self, key):
        self.held[key[1]] = False


def build_program(stop_after=None, rope_eng="pool", dbg=None):
    nc = bass.Bass("TRN2", target_bir_lowering=False)
    R = Rec()
    dbg = dbg or {}

    def din(name, shape, dt=F32):
        return nc.dram_tensor(name, list(shape), dt, kind="ExternalInput").ap()

    x_own = din("x_own", [NOWN, D])
    x_oth = din("x_oth", [NOWN, D])
    memx = din("memx", [256, D])
    w_in = din("w_in", [D, 10240])
    w_mem = din("w_mem", [D, 1024])
    w_br = [din("w_br_a", [256, D]), din("w_br_b", [768, D]), din("w_br_c", [512, D])]
    w_o = din("w_o", [D, D])
    w_fi = din("w_fi", [D, 2 * DFF])
    w_fo = din("w_fo", [DFF, D])
    gcol_d = din("gcol", [128, 48])
    ghead_d = din("ghead", [128, 768])
    rope_own_d = din("rope_own", [128, 16, 128])
    rope_oth_d = din("rope_oth", [128, 16, 128])
    abias_d = din("abias", [128, 24 * 128])
    ident_d = din("ident", [128, 128])
    out_d = nc.dram_tensor("out", [NOWN, D], F32, kind="ExternalOutput").ap()

    def scr(name, shape):
        return nc.dram_tensor(name, list(shape), BF16).ap()

    hT_own_d = scr("hT_own", [16, 128, NOWN])
    QA_d = [scr(f"QA{g}", [2, 128, 2048]) for g in range(3)]
    KA_d = [scr(f"KA{g}", [2, 128, 3072]) for g in range(3)]
    VA_d = [scr(f"VA{g}", [3072, 256]) for g in range(3)]
    QB_d = scr("QB", [6, 128, 2048])
    KB_d = scr("KB", [2, 128, 4096])
    VB_d = scr("VB", [4096, 256])
    QC_d = scr("QC", [4, 128, 2048])
    KC_d = scr("KC", [4, 128, 256])
    VC_d = scr("VC", [256, 512])
    OT_d = scr("OT", [12, 128, NOWN])
    WG_d = scr("WG", [12, 128, 16 * 512])
    WBR_d = scr("WBR", [12, 128, 6 * 512])
    WO_d = scr("WO", [4, 128, 16 * 512])
    WFI_d = scr("WFI", [22, 128, 16 * 512])
    WFO_d = scr("WFO", [16, 128, 11 * 512])

    ARENA = 206 * 1024
    arena = nc.alloc_sbuf_tensor("arena", [128, ARENA], U8).ap()

    class Carver:
        def __init__(self, base):
            self.off = base

        def get(self, shape, dt):
            esz = 4 if dt == F32 else 2
            nb = int(np.prod(shape[1:])) * esz
            nb_al = (nb + 63) // 64 * 64
            assert self.off + nb_al <= ARENA, (self.off, nb_al)
            ap = arena[:, self.off:self.off + nb].bitcast(dt)
            self.off += nb_al
            if len(shape) == 3:
                ap = ap.rearrange("p (a b) -> p a b", a=shape[1])
            return ap

    cv = Carver(0)
    ident = cv.get([128, 128], BF16)
    ones = cv.get([128, 128], BF16)
    gcol = cv.get([128, 48], F32)
    ghead = cv.get([128, 768], F32)
    stat_rot = Rot("st", [cv.get([128, 16], F32) for _ in range(16)], track=True)
    PERS = cv.off

    psb = [nc.alloc_psum_tensor(f"ps{i}", [128, 512], F32).ap() for i in range(8)]

    br_rows = [(0, 2), (2, 6), (8, 4)]
    conv_list = []
    conv_last = {}

    def add_conv(dst, k, src, key):
        for k0 in range(0, k, 4):
            kn = min(4, k - k0)
            conv_last[key] = len(conv_list)
            conv_list.append(lambda k0=k0, kn=kn: R.dma("pool", lambda e: e.dma_start(
                out=dst[:, k0 * 512:(k0 + kn) * 512].rearrange("p (k c) -> p k c", k=kn),
                in_=src[k0 * 128:(k0 + kn) * 128, :].rearrange("(k p) c -> p k c", p=128)),
                writes=[(key, k0)], spool="pool:cv"))

    for cq in range(4):
        for bi in range(3):
            add_conv(WG_d[bi * 4 + cq], 16, w_in[:, 4096 + bi * 2048 + cq * 512:4096 + bi * 2048 + (cq + 1) * 512], ("WG", bi * 4 + cq))
            add_conv(WBR_d[bi * 4 + cq], br_rows[bi][1], w_br[bi][:, cq * 512:(cq + 1) * 512], ("WBR", bi * 4 + cq))
    for cb in range(4):
        add_conv(WO_d[cb], 16, w_o[:, cb * 512:(cb + 1) * 512], ("WO", cb))
    for jg in range(11):
        add_conv(WFI_d[2 * jg], 16, w_fi[:, jg * 512:(jg + 1) * 512], ("WFI", 2 * jg))
        add_conv(WFI_d[2 * jg + 1], 16, w_fi[:, DFF + jg * 512:DFF + (jg + 1) * 512], ("WFI", 2 * jg + 1))
    for cb in range(4):
        for kp in range(4):
            add_conv(WFO_d[cb * 4 + kp], 11, w_fo[kp * 1408:(kp + 1) * 1408, cb * 512:(cb + 1) * 512], ("WFO", cb * 4 + kp))
    conv_pos = [0]

    def pump_conv(n):
        while n > 0 and conv_pos[0] < len(conv_list):
            conv_list[conv_pos[0]]()
            conv_pos[0] += 1
            n -= 1

    def ensure_conv(key):
        while conv_pos[0] <= conv_last[key]:
            pump_conv(1)

    pump_tick = [0]

    def pump_every(n):
        pump_tick[0] += 1
        if pump_tick[0] % n == 0:
            pump_conv(1)

    R.dma("pool", lambda e: e.dma_start(out=ident, in_=ident_d), writes=["ident"])
    R.dve(lambda e: e.memset(ones, 1.0), writes=["ones"])
    R.dma("sp", lambda e: e.dma_start(out=gcol, in_=gcol_d), writes=["gcol"])
    R.dma("sp", lambda e: e.dma_start(out=ghead, in_=ghead_d), writes=["ghead"])

    def run_pipeline(units):
        active = []
        it = iter(units)
        pending = True
        while pending or active:
            nxt = []
            for gen in active:
                try:
                    next(gen)
                    nxt.append(gen)
                except StopIteration:
                    pass
            active = nxt
            if pending:
                try:
                    u = next(it)
                    try:
                        next(u)
                        active.append(u)
                    except StopIteration:
                        pass
                except StopIteration:
                    pending = False

    cv = Carver(PERS)
    rope_own = cv.get([128, 16, 128], F32)
    rope_oth = cv.get([128, 16, 128], F32)
    hT_bufs = [cv.get([128, 16, 1024], BF16) for _ in range(2)]
    xin_rot = Rot("xin", [cv.get([128, D], F32) for _ in range(2)], track=True)
    xs_rot = Rot("xs", [cv.get([128, D], BF16) for _ in range(2)], track=True)
    junk = cv.get([128, D], BF16)
    wblk_rot = Rot("wblk", [cv.get([128, 16, 512], BF16) for _ in range(3)])
    qf_rot = Rot("qf", [cv.get([128, 512], F32) for _ in range(10)], track=True)
    rt_rot = Rot("rt", [cv.get([128, 1024], F32) for _ in range(2)])
    qb_rot = Rot("qb16", [cv.get([128, 512], BF16) for _ in range(8)], track=True)
    sqb_rot = Rot("sqb", [cv.get([128, 512], BF16) for _ in range(3)], track=True)
    stg_rot = Rot("stg", [cv.get([128, 512], BF16) for _ in range(2)])
    vstg_rot = Rot("vstg", [cv.get([128, 512], BF16) for _ in range(2)])
    G1_END = cv.off

    R.dma("sp", lambda e: e.dma_start(out=rope_own, in_=rope_own_d), writes=["rope_own"])
    R.dma("sp", lambda e: e.dma_start(out=rope_oth, in_=rope_oth_d), writes=["rope_oth"])

    psA_rot = Rot("psA", [(i, psb[i]) for i in range(0, 5)], track=True)
    pst_rot = Rot("pst", [(i, psb[i].bitcast(BF16)) for i in (5, 6, 7)])

    def next_ps(rot):
        (i, ap), _ = rot.next()
        return ap, ("ps", i)

    def next_ps_t(rot):
        (i, ap), rk = rot.next()
        return ap, ("ps", i), rk

    def rstd_stage_ms(st, stk, ntok, n, inv_n, c_ss, tmp):
        R.dve(lambda e: e.tensor_scalar(st[:ntok, tmp:tmp + n], st[:ntok, c_ss:c_ss + n], inv_n, EPS, ALU.mult, ALU.add),
              reads=[stk], writes=[stk])

    def rstd_stage_lnexp(st, stk, ntok, n, tmp, c_out):
        R.act(lambda e: e.activation(st[:ntok, tmp:tmp + n], st[:ntok, tmp:tmp + n], AF.Ln), reads=[stk], writes=[stk])
        R.act(lambda e: e.activation(st[:ntok, c_out:c_out + n], st[:ntok, tmp:tmp + n], AF.Exp, scale=-0.5),
              reads=[stk], writes=[stk])

    def norm_unit(load_fn, x_ap, x_key, goff, hT, hT_key, col0, pr, xsr, jk):
        loaded = load_fn is not None
        if loaded:
            x_ap, x_key = load_fn()
            yield
        st, stk = stat_rot.next()
        R.act(lambda e: e.activation(jk, x_ap, AF.Square, accum_out=st[:, 0:1]), reads=[x_key], writes=["junk", stk])
        yield
        rstd_stage_ms(st, stk, 128, 1, 1.0 / D, 0, 5)
        yield
        rstd_stage_lnexp(st, stk, 128, 1, 5, 1)
        xs, xsk = xsr.next()
        R.act(lambda e: e.activation(xs, x_ap, AF.Copy, scale=st[:, 1:2]), reads=[x_key, stk], writes=[xsk])
        stat_rot.release(stk)
        if loaded:
            xin_rot.release(x_key)
        yield
        for half in range(2):
            pt, ptk = next_ps(pr)
            for q in range(8):
                kc = half * 8 + q
                R.pe(lambda e, pt=pt, q=q, kc=kc: e.transpose(pt[:, q * 128:(q + 1) * 128], xs[:, kc * 128:(kc + 1) * 128], ident),
                     reads=[xsk, "ident"], writes=[ptk])
            R.dve(lambda e, pt=pt, half=half: e.tensor_tensor(
                hT[:, half * 8:half * 8 + 8, col0:col0 + 128], pt.rearrange("p (a b) -> p a b", a=8),
                gcol[:, goff + half * 8:goff + half * 8 + 8].unsqueeze(2).to_broadcast([128, 8, 128]), ALU.mult),
                reads=[ptk, "gcol"], writes=[hT_key])
        xsr.release(xsk)

    def load_unit(job):
        wb, wk = wblk_rot.next()
        off = 0
        for wap in job["wsegs"]:
            ncols = wap.shape[1]
            R.dma("pool", lambda e, wap=wap, off=off, ncols=ncols: e.dma_start(
                out=wb[:, :, off:off + ncols], in_=wap.rearrange("(k p) c -> p k c", p=128)), writes=[wk])
            off += ncols
        job["wb"], job["wk"], job["W"] = wb, wk, off
        return
        yield

    def proj_unit(job, hT, hT_key, csl, ntok, info):
        wb, wk, W = job["wb"], job["wk"], job["W"]
        jk = junk
        ps, psk, psrk = next_ps_t(psA_rot)
        for kc in range(16):
            R.pe(lambda e, kc=kc: e.matmul(ps[:ntok, :W], hT[:, kc, csl], wb[:, kc, :W], start=(kc == 0), stop=(kc == 15)),
                 reads=[hT_key, wk], writes=[psk])
        pump_every(3)
        yield
        qs = []
        for (coff, ncols, kind, gi, rope_fn, dest_fn) in job["segs"]:
            if kind == "v":
                vs, vsk = vstg_rot.next()
                R.act(lambda e, vs=vs, coff=coff, ncols=ncols: e.copy(vs[:ntok, :ncols], ps[:ntok, coff:coff + ncols]),
                      reads=[psk], writes=[vsk])
                dram_ap, dkey = dest_fn(info, ntok)
                R.dma("sp", lambda e, vs=vs, dram_ap=dram_ap, ncols=ncols: e.dma_start(out=dram_ap, in_=vs[:ntok, :ncols]),
                      reads=[vsk], writes=[dkey])
            else:
                nh = ncols // 128
                qf, qfk = qf_rot.next()
                R.act(lambda e, qf=qf, coff=coff, ncols=ncols: e.copy(qf[:ntok, :ncols], ps[:ntok, coff:coff + ncols]),
                      reads=[psk], writes=[qfk])
                qs.append([coff, nh, gi, rope_fn, dest_fn, None, None, qf, qfk])
        psA_rot.release(psrk)
        if not qs:
            return
        yield
        for q in qs:
            coff, nh, gi, rope_fn, dest_fn, _, _, qf, qfk = q
            st, stk = stat_rot.next()
            q[5], q[6] = st, stk
            if rope_fn is not None:
                for h in range(nh):
                    R.act(lambda e, st=st, h=h, qf=qf: e.activation(
                        jk[:ntok, h * 128:(h + 1) * 128], qf[:ntok, h * 128:(h + 1) * 128], AF.Square,
                        accum_out=st[:ntok, h:h + 1]), reads=[qfk], writes=["junk", stk])
                q.append(None)
            else:
                sqb, sqk = sqb_rot.next()
                R.pool(lambda e, sqb=sqb, qf=qf, nh=nh: e.tensor_tensor(
                    sqb[:ntok, :nh * 128], qf[:ntok, :nh * 128], qf[:ntok, :nh * 128], ALU.mult), reads=[qfk], writes=[sqk])
                q.append((sqb, sqk))
        yield
        for q in qs:
            coff, nh, gi, rope_fn, dest_fn, st, stk, qf, qfk, sq = q
            if sq is not None:
                sqb, sqk = sq
                R.dve(lambda e, st=st, sqb=sqb, nh=nh: e.tensor_reduce(
                    st[:ntok, 0:nh], sqb[:ntok, :nh * 128].rearrange("p (h d) -> p h d", h=nh), AX.X, ALU.add),
                    reads=[sqk], writes=[stk])
                sqb_rot.release(sqk)
        qs = [tuple(q[:9]) for q in qs]
        for (coff, nh, gi, rope_fn, dest_fn, st, stk, qf, qfk) in qs:
            rstd_stage_ms(st, stk, ntok, nh, 1.0 / 128, 0, 12)
        yield
        for (coff, nh, gi, rope_fn, dest_fn, st, stk, qf, qfk) in qs:
            rstd_stage_lnexp(st, stk, ntok, nh, 12, 8)
        yield
        outs = []
        for (coff, nh, gi, rope_fn, dest_fn, st, stk, qf, qfk) in qs:
            qb, qbk = qb_rot.next()
            rope = rope_fn(info) if rope_fn is not None else None
            if rope is None:
                dst, dkeys = qb, [(qbk, 0), (qbk, 1)]
            else:
                dst, dkeys = qf, [qfk]
            for h in range(nh):
                R.dve(lambda e, st=st, h=h, qf=qf, dst=dst, gi=gi: e.scalar_tensor_tensor(
                    dst[:ntok, h * 128:(h + 1) * 128], qf[:ntok, h * 128:(h + 1) * 128], st[:ntok, 8 + h:9 + h],
                    ghead[:ntok, gi * 128:(gi + 1) * 128], ALU.mult, ALU.mult),
                    reads=[qfk, stk, "ghead"], writes=dkeys)
            stat_rot.release(stk)
            if rope is None:
                qf_rot.release(qfk)
            outs.append((nh, dest_fn, qb, qbk, qf, qfk, rope))
        yield
        any_rope = False
        for (nh, dest_fn, qb, qbk, qf, qfk, rope) in outs:
            if rope is None:
                continue
            any_rope = True
            rp, rpk = rope
            W2 = nh * 128
            rt, rtk = rt_rot.next()
            qf3 = qf[:ntok, :W2].rearrange("p (h d) -> p h d", h=nh)
            qb3 = qb[:ntok, :W2].rearrange("p (h d) -> p h d", h=nh)
            x0 = qf3[:, :, 0::2]
            x1 = qf3[:, :, 1::2]
            cs = rp[:ntok, 0:64].unsqueeze(1).to_broadcast([ntok, nh, 64])
            sn = rp[:ntok, 64:128].unsqueeze(1).to_broadcast([ntok, nh, 64])
            t = rt[:ntok, :2 * W2].rearrange("p (a h d) -> p a h d", a=4, h=nh)
            R.op(rope_eng, lambda e, t=t, x0=x0, cs=cs: e.tensor_tensor(t[:, 0], x0, cs, ALU.mult), reads=[qfk, rpk], writes=[(rtk, 0)])
            R.op(rope_eng, lambda e, t=t, x1=x1, sn=sn: e.tensor_tensor(t[:, 1], x1, sn, ALU.mult), reads=[qfk, rpk], writes=[(rtk, 1)])
            R.op(rope_eng, lambda e, t=t, x0=x0, sn=sn: e.tensor_tensor(t[:, 2], x0, sn, ALU.mult), reads=[qfk, rpk], writes=[(rtk, 2)])
            R.op(rope_eng, lambda e, t=t, x1=x1, cs=cs: e.tensor_tensor(t[:, 3], x1, cs, ALU.mult), reads=[qfk, rpk], writes=[(rtk, 3)])
            R.op(rope_eng, lambda e, t=t, qb3=qb3: e.tensor_tensor(qb3[:, :, 0::2], t[:, 0], t[:, 1], ALU.subtract),
                   reads=[(rtk, 0), (rtk, 1)], writes=[(qbk, 0)])
            R.op(rope_eng, lambda e, t=t, qb3=qb3: e.tensor_tensor(qb3[:, :, 1::2], t[:, 2], t[:, 3], ALU.add),
                   reads=[(rtk, 2), (rtk, 3)], writes=[(qbk, 1)])
            qf_rot.release(qfk)
        if any_rope:
            yield
        for (nh, dest_fn, qb, qbk, qf, qfk, rope) in outs:
            pt, ptk = next_ps(pst_rot)
            for h in range(nh):
                R.pe(lambda e, h=h, pt=pt, qb=qb: e.transpose(pt[:, h * ntok:(h + 1) * ntok], qb[:ntok, h * 128:(h + 1) * 128],
                                                              ident[:ntok, :ntok]),
                     reads=[(qbk, 0), (qbk, 1), "ident"], writes=[ptk])
            qb_rot.release(qbk)
            sg, sgk = stg_rot.next()
            R.act(lambda e, sg=sg, pt=pt, nh=nh: e.copy(sg[:, :nh * ntok], pt[:, :nh * ntok]), reads=[ptk], writes=[sgk])
            dram_ap, dkey = dest_fn(info, ntok)
            R.dma("sp", lambda e, sg=sg, dram_ap=dram_ap, nh=nh: e.dma_start(
                out=dram_ap.rearrange("h p t -> p h t"), in_=sg[:, :nh * ntok].rearrange("p (h t) -> p h t", h=nh)),
                reads=[sgk], writes=[dkey])

    def a_tiles(chunk_kind, k):
        res = {}
        for g, d in enumerate(DILS):
            Lo, Lh = NOWN // d, 1024 // d
            Lr = Lo + Lh
            per = 1024 // d
            ntok = min(128, per)
            tl = []
            for cl in range(d):
                for mt in range(per // ntok):
                    start = cl + d * mt * ntok
                    stop = min(start + d * ntok, 1024)
                    csl = slice(start, stop, d) if d > 1 else slice(start, start + ntok)
                    if chunk_kind == "own":
                        m0 = k * per + mt * ntok
                        tl.append((csl, ntok, (cl * Lo + m0, cl * Lr + m0)))
                    else:
                        tl.append((csl, ntok, (None, cl * Lr + Lo + mt * ntok)))
            res[g] = tl
        return res

    def wcols(w, c0, n):
        return w[:, c0:c0 + n]

    nat_tiles = [(slice(i * 128, (i + 1) * 128), 128, i) for i in range(8)]

    def chunk_jobs(kind, k):
        jobs = []
        if kind == "mem":
            jobs.append(dict(wsegs=[wcols(w_mem, 0, 512)], tiles=nat_tiles[:2],
                             segs=[(0, 512, "qk", 5, None, lambda i, n: (KC_d[:, :, i * 128:(i + 1) * 128], "KC_d"))]))
            jobs.append(dict(wsegs=[wcols(w_mem, 512, 512)], tiles=nat_tiles[:2],
                             segs=[(0, 512, "v", 0, None, lambda i, n: (VC_d[i * 128:(i + 1) * 128, :], "VC_d"))]))
            return jobs
        if kind == "own":
            base = k * 1024
            rp = lambda i: (rope_own[:, k * 8 + i, :], "rope_own")
            cols = lambda i: slice(base + i * 128, base + (i + 1) * 128)
            jobs.append(dict(wsegs=[wcols(w_in, 2304, 512)], tiles=nat_tiles,
                             segs=[(0, 512, "qk", 2, rp, lambda i, n: (QB_d[0:4, :, cols(i)], "QB_d"))]))
            jobs.append(dict(wsegs=[wcols(w_in, 2816, 512)], tiles=nat_tiles,
                             segs=[(0, 256, "qk", 2, rp, lambda i, n: (QB_d[4:6, :, cols(i)], "QB_d")),
                                   (256, 256, "qk", 3, rp, lambda i, n: (KB_d[0:2, :, cols(i)], "KB_d"))]))
            jobs.append(dict(wsegs=[wcols(w_in, 3328, 512)], tiles=nat_tiles,
                             segs=[(0, 256, "v", 0, None, lambda i, n: (VB_d[cols(i), :], "VB_d")),
                                   (256, 256, "qk", 4, None, lambda i, n: (QC_d[0:2, :, cols(i)], "QC_d"))]))
            jobs.append(dict(wsegs=[wcols(w_in, 3840, 256)], tiles=nat_tiles,
                             segs=[(0, 256, "qk", 4, None, lambda i, n: (QC_d[2:4, :, cols(i)], "QC_d"))]))
            at = a_tiles("own", k)
            for g in range(3):
                jobs.append(dict(wsegs=[wcols(w_in, g * 256, 256), wcols(w_in, 768 + g * 256, 256)], tiles=at[g],
                                 segs=[(0, 256, "qk", 0, None, lambda inf, n, g=g: (QA_d[g][:, :, inf[0]:inf[0] + n], ("QA_d", g))),
                                       (256, 256, "qk", 1, None, lambda inf, n, g=g: (KA_d[g][:, :, inf[1]:inf[1] + n], ("KA_d", g)))]))
                jobs.append(dict(wsegs=[wcols(w_in, 1536 + g * 256, 256)], tiles=at[g],
                                 segs=[(0, 256, "v", 0, None, lambda inf, n, g=g: (VA_d[g][inf[1]:inf[1] + n, :], ("VA_d", g)))]))
            return jobs
        base = 2048 + k * 1024
        rp = lambda i: (rope_oth[:, k * 8 + i, :], "rope_oth")
        cols = lambda i: slice(base + i * 128, base + (i + 1) * 128)
        jobs.append(dict(wsegs=[wcols(w_in, 3072, 512)], tiles=nat_tiles,
                         segs=[(0, 256, "qk", 3, rp, lambda i, n: (KB_d[0:2, :, cols(i)], "KB_d")),
                               (256, 256, "v", 0, None, lambda i, n: (VB_d[cols(i), :], "VB_d"))]))
        if k == 0:
            at = a_tiles("halo", 0)
            for g in range(3):
                jobs.append(dict(wsegs=[wcols(w_in, 768 + g * 256, 256), wcols(w_in, 1536 + g * 256, 256)], tiles=at[g],
                                 segs=[(0, 256, "qk", 1, None, lambda inf, n, g=g: (KA_d[g][:, :, inf[1]:inf[1] + n], ("KA_d", g))),
                                       (256, 256, "v", 0, None, lambda inf, n, g=g: (VA_d[g][inf[1]:inf[1] + n, :], ("VA_d", g)))]))
        return jobs

    def s1_units(kind, k, hT, hT_key):
        us = []
        ntiles = 2 if kind == "mem" else 8
        for t in range(ntiles):
            if kind == "mem":
                src, goff = memx[t * 128:(t + 1) * 128, :], 32
            elif kind == "own":
                src, goff = x_own[k * 1024 + t * 128:k * 1024 + (t + 1) * 128, :], 0
            else:
                src, goff = x_oth[k * 1024 + t * 128:k * 1024 + (t + 1) * 128, :], 0

            def load_fn(src=src):
                xin, xk = xin_rot.next()
                R.dma("sp", lambda e: e.dma_start(out=xin, in_=src), writes=[xk])
                return xin, xk
            gen = norm_unit(load_fn, None, None, goff, hT, hT_key, t * 128, pst_rot, xs_rot, junk)
            if kind == "own" and k == 1 and t == 5 and "cut" in dbg:
                import itertools
                gen = itertools.islice(gen, dbg["cut"])
            us.append(gen)
        return us

    def nop_unit():
        return
        yield

    def store_hT_unit(k, hT, hT_key):
        for _ in range(8):
            yield
        R.dma("sp", lambda e: e.dma_start(out=hT_own_d[:, :, k * 1024:(k + 1) * 1024].rearrange("k p t -> p k t"), in_=hT),
              reads=[hT_key], writes=["hT_own_d"])
        return
        yield

    chunks = [("mem", 0), ("oth", 0), ("oth", 1), ("own", 0), ("own", 1)]
    all_jobs = []
    for ci, (kind, k) in enumerate(chunks):
        for job in chunk_jobs(kind, k):
            job["ci"] = ci
            all_jobs.append(job)
    units = []
    LBL = {}
    units += s1_units("mem", 0, hT_bufs[0], ("hT", 0))
    units.append(load_unit(all_jobs[0]))
    ji = 0
    for ci, (kind, k) in enumerate(chunks):
        hT, hT_key = hT_bufs[ci % 2], ("hT", ci % 2)
        prim = []
        while ji < len(all_jobs) and all_jobs[ji]["ci"] == ci:
            job = all_jobs[ji]
            if ji + 1 < len(all_jobs):
                prim.append(load_unit(all_jobs[ji + 1])); LBL[id(prim[-1])] = ('load', ji + 1)
            for (csl, ntok, info) in job["tiles"]:
                prim.append(proj_unit(job, hT, hT_key, csl, ntok, info)); LBL[id(prim[-1])] = ('proj', ji, str(csl), ntok, str(info))
            ji += 1
        sec = []
        if ci + 1 < len(chunks):
            nk, nkk = chunks[ci + 1]
            sec = s1_units(nk, nkk, hT_bufs[(ci + 1) % 2], ("hT", (ci + 1) % 2))
            if nk == "own":
                sec += [None] * 8 + [store_hT_unit(nkk, hT_bufs[(ci + 1) % 2], ("hT", (ci + 1) % 2))]
        merged = []
        np_, ns_ = len(prim), len(sec)
        si = 0
        for pi, u in enumerate(prim):
            merged.append(u)
            while si < ns_ and (si + 1) * np_ <= (pi + 1) * (ns_ + 1) * 0.8 + 1e-9 and si < ns_:
                if sec[si] is not None:
                    merged.append(sec[si])
                si += 1
                break
        while si < ns_:
            if sec[si] is not None:
                merged.append(sec[si])
                merged.append(nop_unit())
            si += 1
        units += merged
    if dbg.get('print'):
        for i, u in enumerate(units):
            print(i, LBL.get(id(u), 's1/store'))
    run_pipeline([u for i, u in enumerate(units[:dbg.get('max_units', len(units))]) if i not in dbg.get('skip', ())])

    R.barrier()
    if stop_after == 1:
        R.disabled = True

    cv = Carver(PERS)
    abias = cv.get([128, 24 * 128], F32)
    Qa = cv.get([128, 2, 2048], BF16)
    Ka = cv.get([128, 2, 3072], BF16)
    Va = cv.get([128, 32, 256], BF16)
    Uacc = cv.get([128, 2, 2048], F32)
    Lacc = cv.get([128, 2, 2048], F32)
    sb_rot = Rot("sb", [cv.get([128, 384], F32) for _ in range(4)])
    P_rot = Rot("P", [cv.get([128, 384], BF16) for _ in range(4)])
    rl_a = Qa.rearrange("p h n -> p (h n)").bitcast(F32)
    oa_bf = Ka[:, 0, 0:2048]

    R.dma("sp", lambda e: e.dma_start(out=abias, in_=abias_d), writes=["abias"])
    psS_rot = Rot("ps", [(i, psb[i]) for i in range(0, 4)])
    psOL_rot = Rot("ps", [(4, 5), (6, 7)])

    def a_unit(g, h, qcol, segs, asl):
        S, Sk = next_ps(psS_rot)
        for si, (kcol, nk_s, vap, bt) in enumerate(segs):
            R.pe(lambda e, si=si, kcol=kcol, nk_s=nk_s: e.matmul(
                S[0:nk_s, si * 128:(si + 1) * 128], Ka[:, h, kcol:kcol + nk_s], Qa[:, h, qcol:qcol + 128],
                start=True, stop=True), reads=["Ka", "Qa"], writes=[Sk])
        yield
        sb, sbk = sb_rot.next()
        for si, (kcol, nk_s, vap, bt) in enumerate(segs):
            R.dve(lambda e, si=si, nk_s=nk_s, bt=bt: e.scalar_tensor_tensor(
                sb[0:nk_s, si * 128:(si + 1) * 128], S[0:nk_s, si * 128:(si + 1) * 128], SCALE,
                abias[0:nk_s, bt * 128:(bt + 1) * 128], ALU.mult, ALU.add),
                reads=[Sk, "abias"], writes=[(sbk, si)])
        yield
        Pt, Pk = P_rot.next()
        for si, (kcol, nk_s, vap, bt) in enumerate(segs):
            R.act(lambda e, si=si, nk_s=nk_s: e.activation(
                Pt[0:nk_s, si * 128:(si + 1) * 128], sb[0:nk_s, si * 128:(si + 1) * 128], AF.Exp),
                reads=[(sbk, si)], writes=[(Pk, si)])
        yield
        (io, il), _ = psOL_rot.next()
        O, Ok = psb[io], ("ps", io)
        L, Lk = psb[il], ("ps", il)
        ns = len(segs)
        for si, (kcol, nk_s, vap, bt) in enumerate(segs):
            R.pe(lambda e, si=si, nk_s=nk_s, vap=vap: e.matmul(
                O[:, 0:128], vap, Pt[0:nk_s, si * 128:(si + 1) * 128], start=(si == 0), stop=(si == ns - 1)),
                reads=["Va", (Pk, si)], writes=[Ok])
        for si, (kcol, nk_s, vap, bt) in enumerate(segs):
            R.pe(lambda e, si=si, nk_s=nk_s: e.matmul(
                L[:, 0:128], ones[0:nk_s, :], Pt[0:nk_s, si * 128:(si + 1) * 128], start=(si == 0), stop=(si == ns - 1)),
                reads=["ones", (Pk, si)], writes=[Lk])
        yield
        ukey = ("Uacc", h)
        lkey = ("Lacc", h)
        if g == 0:
            R.dve(lambda e: e.tensor_copy(Uacc[:, h, asl], O[:, 0:128]), reads=[Ok], writes=[ukey])
            R.dve(lambda e: e.tensor_copy(Lacc[:, h, asl], L[:, 0:128]), reads=[Lk], writes=[lkey])
        else:
            R.dve(lambda e: e.tensor_tensor(Uacc[:, h, asl], Uacc[:, h, asl], O[:, 0:128], ALU.add), reads=[Ok, ukey], writes=[ukey])
            R.dve(lambda e: e.tensor_tensor(Lacc[:, h, asl], Lacc[:, h, asl], L[:, 0:128], ALU.add), reads=[Lk, lkey], writes=[lkey])

    for g, d in enumerate(DILS):
        Lo, Lh = NOWN // d, 1024 // d
        Lr = Lo + Lh
        R.dma("sp", lambda e, g=g: e.dma_start(out=Qa, in_=QA_d[g].rearrange("h p n -> p h n")), reads=[("QA_d", g)], writes=["Qa"])
        R.dma("sp", lambda e, g=g: e.dma_start(out=Ka, in_=KA_d[g].rearrange("h p n -> p h n")), reads=[("KA_d", g)], writes=["Ka"])
        if g < 2:
            R.dma("sp", lambda e, g=g: e.dma_start(out=Va[:, 0:24, :], in_=VA_d[g].rearrange("(t p) c -> p t c", p=128)),
                  reads=[("VA_d", g)], writes=["Va"])
        else:
            vv = VA_d[g].rearrange("(c r) x -> c r x", r=192)
            R.dma("sp", lambda e, vv=vv: e.dma_start(out=Va[:, 0:16, :], in_=vv[:, 0:128, :].rearrange("c p x -> p c x")),
                  reads=[("VA_d", g)], writes=["Va"])
            R.dma("sp", lambda e, vv=vv: e.dma_start(out=Va[0:64, 16:32, :], in_=vv[:, 128:192, :].rearrange("c p x -> p c x")),
                  reads=[("VA_d", g)], writes=["Va"])
        aunits = []
        for h in range(2):
            if g < 2:
                nq, nk = Lo // 128, Lr // 128
                bbase = (g * 2 + h) * 5
                for cl in range(d):
                    for qi in range(nq):
                        segs = []
                        for r, kt in enumerate(((qi - 1) % nk, qi, (qi + 1) % nk)):
                            bt = bbase + r
                            if r == 0 and qi == 0:
                                bt = bbase + 3
                            if r == 2 and qi == nq - 1:
                                bt = bbase + 4
                            segs.append((cl * Lr + kt * 128, 128, Va[:, cl * nk + kt, h * 128:(h + 1) * 128], bt))
                        a0 = cl + d * qi * 128
                        asl = slice(a0, a0 + 128) if d == 1 else slice(a0, min(a0 + d * 128, 2048), d)
                        aunits.append(a_unit(g, h, cl * Lo + qi * 128, segs, asl))
            else:
                bbase = 20 + h * 2
                for cl in range(16):
                    segs = [(cl * 192, 128, Va[:, cl, h * 128:(h + 1) * 128], bbase),
                            (cl * 192 + 128, 64, Va[0:64, 16 + cl, h * 128:(h + 1) * 128], bbase + 1)]
                    aunits.append(a_unit(g, h, cl * 128, segs, slice(cl, 2048, 16)))
        run_pipeline(aunits)
    for h in range(2):
        R.dve(lambda e, h=h: e.reciprocal(rl_a, Lacc[:, h, :]), reads=[("Lacc", h)], writes=["Qa"])
        R.dve(lambda e, h=h: e.tensor_tensor(oa_bf, Uacc[:, h, :], rl_a, ALU.mult), reads=[("Uacc", h), "Qa"], writes=["Ka"])
        R.dma("sp", lambda e, h=h: e.dma_start(out=OT_d[h], in_=oa_bf), reads=["Ka"], writes=["OT_d"])

    KBs = cv.get([128, 2, 4096], BF16)
    VBs = cv.get([128, 32, 256], BF16)
    QBs = cv.get([128, 6, 2048], BF16)
    KCs = cv.get([128, 4, 256], BF16)
    VCs = cv.get([128, 2, 512], BF16)
    QCs = cv.get([128, 4, 2048], BF16)
    P2_rot = Rot("P2", [cv.get([128, 512], BF16) for _ in range(4)])
    rl_rot = Rot("rl", [cv.get([128, 512], F32) for _ in range(2)])
    ob_rot = Rot("ob", [cv.get([128, 512], BF16) for _ in range(2)])

    R.dma("sp", lambda e: e.dma_start(out=KBs, in_=KB_d.rearrange("h p n -> p h n")), reads=["KB_d"], writes=["KBs"])
    R.dma("sp", lambda e: e.dma_start(out=VBs, in_=VB_d.rearrange("(t p) c -> p t c", p=128)), reads=["VB_d"], writes=["VBs"])
    R.dma("sp", lambda e: e.dma_start(out=QBs, in_=QB_d.rearrange("h p n -> p h n")), reads=["QB_d"], writes=["QBs"])
    R.dma("sp", lambda e: e.dma_start(out=KCs, in_=KC_d.rearrange("h p n -> p h n")), reads=["KC_d"], writes=["KCs"])
    R.dma("sp", lambda e: e.dma_start(out=VCs, in_=VC_d.rearrange("(t p) c -> p t c", p=128)), reads=["VC_d"], writes=["VCs"])
    R.dma("sp", lambda e: e.dma_start(out=QCs, in_=QC_d.rearrange("h p n -> p h n")), reads=["QC_d"], writes=["QCs"])

    psS2_rot = Rot("ps", [(i, psb[i]) for i in range(0, 4)])
    psOL2_rot = Rot("ps", [(4, 5), (6, 7)])

    def dense_attn(Ks, Kkey, kh, Vs, Vkey, vcol, nkt, Qs, Qkey, qh, qb, ot_idx):
        q_ap = Qs[:, qh, qb * 512:(qb + 1) * 512]
        (io, il), _ = psOL2_rot.next()
        O, Ok = psb[io], ("ps", io)
        L, Lk = psb[il], ("ps", il)
        pend = []

        def issue_s(kt):
            S, Sk = next_ps(psS2_rot)
            R.pe(lambda e, S=S, kt=kt: e.matmul(S, Ks[:, kh, kt * 128:(kt + 1) * 128], q_ap, start=True, stop=True),
                 reads=[Kkey, Qkey], writes=[Sk])
            Pt, Pk = P2_rot.next()
            R.act(lambda e, S=S, Pt=Pt: e.activation(Pt, S, AF.Exp, scale=SCALE), reads=[Sk], writes=[Pk])
            pend.append((Pt, Pk))

        issue_s(0)
        if nkt > 1:
            issue_s(1)
        for kt in range(nkt):
            if kt + 2 < nkt:
                issue_s(kt + 2)
            pump_every(8)
            Pt, Pk = pend.pop(0)
            R.pe(lambda e, Pt=Pt, kt=kt: e.matmul(O, Vs[:, kt, vcol:vcol + 128], Pt, start=(kt == 0), stop=(kt == nkt - 1)),
                 reads=[Vkey, Pk], writes=[Ok])
            R.pe(lambda e, Pt=Pt, kt=kt: e.matmul(L, ones, Pt, start=(kt == 0), stop=(kt == nkt - 1)),
                 reads=["ones", Pk], writes=[Lk])
        rl, rlk = rl_rot.next()
        ob, obk = ob_rot.next()
        R.dve(lambda e: e.reciprocal(rl, L), reads=[Lk], writes=[rlk])
        R.dve(lambda e: e.tensor_tensor(ob, O, rl, ALU.mult), reads=[Ok, rlk], writes=[obk])
        R.dma("sp", lambda e: e.dma_start(out=OT_d[ot_idx, :, qb * 512:(qb + 1) * 512], in_=ob), reads=[obk], writes=["OT_d"])

    for j in range(2):
        for qb in range(4):
            for gq in range(3):
                h = 3 * j + gq
                dense_attn(KBs, "KBs", j, VBs, "VBs", j * 128, 32, QBs, "QBs", h, qb, 2 + h)
    for h in range(4):
        for qb in range(4):
            dense_attn(KCs, "KCs", h, VCs, "VCs", h * 128, 2, QCs, "QCs", h, qb, 8 + h)

    R.barrier()

    cv = Carver(PERS)
    x1 = cv.get([128, 4, D], F32)
    hTb = cv.get([128, 16, 512], BF16)
    uT = cv.get([128, 44, 512], BF16)
    OTb = cv.get([128, 12, 512], BF16)
    mT = uT[:, 12:28, :]
    wblk_rot = Rot("wblk", [cv.get([128, 16, 512], BF16) for _ in range(4)])
    xs_rot = Rot("xs", [cv.get([128, D], BF16) for _ in range(1)])
    junk = cv.get([128, D], BF16)
    sg_rot = Rot("sgm", [cv.get([128, 512], F32) for _ in range(2)])
    maccs = [cv.get([128, 512], F32) for _ in range(4)]
    tmp_rot = Rot("tmp", [cv.get([128, 512], F32) for _ in range(2)])
    ostg_rot = Rot("ostg", [cv.get([128, 512], F32) for _ in range(2)])

    psG_rot = Rot("ps", [(i, psb[i]) for i in range(0, 8)])
    pst4_rot = Rot("ps", [(6, psb[6].bitcast(BF16)), (7, psb[7].bitcast(BF16))])

    def load_w(src_blk, k, key):
        ensure_conv(key)
        wb, wk = wblk_rot.next()
        R.dma("pool", lambda e: e.dma_start(out=wb[:, 0:k, :], in_=src_blk[:, 0:k * 512].rearrange("p (k c) -> p k c", k=k)),
              reads=[(key, k0) for k0 in range(0, k, 4)], writes=[wk])
        return wb, wk

    def load_block_acts(t0):
        R.dma("sp", lambda e: e.dma_start(out=hTb, in_=hT_own_d[:, :, t0:t0 + 512].rearrange("k p t -> p k t")),
              reads=["hT_own_d"], writes=["hTb"])
        R.dma("sp", lambda e: e.dma_start(out=OTb, in_=OT_d[:, :, t0:t0 + 512].rearrange("k p t -> p k t")),
              reads=["OT_d"], writes=[("OTb", i) for i in range(12)])

    load_block_acts(0)
    for tb in range(4):
        t0 = tb * 512
        R.dma("sp", lambda e, t0=t0: e.dma_start(out=x1, in_=x_own[t0:t0 + 512, :].rearrange("(t p) c -> p t c", p=128)),
              writes=[("x1", t) for t in range(4)])
        for cq in range(4):
            for bi in range(3):
                f0, nch = br_rows[bi]
                wg, wgk = load_w(WG_d[bi * 4 + cq], 16, ("WG", bi * 4 + cq))
                wbr, wbrk = load_w(WBR_d[bi * 4 + cq], nch, ("WBR", bi * 4 + cq))
                for cc in range(4):
                    c = cq * 4 + cc
                    macc = maccs[cc]
                    mk = ("macc", cc)
                    G, Gk = next_ps(psG_rot)
                    for kc in range(16):
                        R.pe(lambda e, G=G, wg=wg, kc=kc, cc=cc: e.matmul(
                            G, wg[:, kc, cc * 128:(cc + 1) * 128], hTb[:, kc, :], start=(kc == 0), stop=(kc == 15)),
                            reads=[wgk, "hTb"], writes=[Gk])
                    Bp, Bk = next_ps(psG_rot)
                    for i in range(nch):
                        R.pe(lambda e, Bp=Bp, wbr=wbr, i=i, f0=f0, nch=nch, cc=cc: e.matmul(
                            Bp, wbr[:, i, cc * 128:(cc + 1) * 128], OTb[:, f0 + i, :], start=(i == 0), stop=(i == nch - 1)),
                            reads=[wbrk] + [("OTb", f0 + i)], writes=[Bk])
                    pump_conv(1)
                    sg, sgk = sg_rot.next()
                    R.act(lambda e, sg=sg, G=G: e.activation(sg, G, AF.Sigmoid), reads=[Gk], writes=[sgk])
                    if bi == 0:
                        R.dve(lambda e, sg=sg, Bp=Bp, macc=macc: e.tensor_tensor(macc, sg, Bp, ALU.mult), reads=[sgk, Bk], writes=[mk])
                    elif bi == 1:
                        tm, tmk = tmp_rot.next()
                        R.dve(lambda e, sg=sg, Bp=Bp, tm=tm: e.tensor_tensor(tm, sg, Bp, ALU.mult), reads=[sgk, Bk], writes=[tmk])
                        R.dve(lambda e, tm=tm, macc=macc: e.tensor_tensor(macc, macc, tm, ALU.add), reads=[tmk, mk], writes=[mk])
                    else:
                        tm, tmk = tmp_rot.next()
                        R.dve(lambda e, sg=sg, Bp=Bp, tm=tm: e.tensor_tensor(tm, sg, Bp, ALU.mult), reads=[sgk, Bk], writes=[tmk])
                        R.dve(lambda e, tm=tm, c=c, macc=macc: e.tensor_tensor(mT[:, c, :], macc, tm, ALU.add),
                              reads=[tmk, mk], writes=[("uT", 12 + c)])
        for cb in range(4):
            wb, wk = load_w(WO_d[cb], 16, ("WO", cb))
            for tt in range(4):
                ps, psk = next_ps(psG_rot)
                for c in range(16):
                    R.pe(lambda e, ps=ps, wb=wb, c=c, tt=tt: e.matmul(
                        ps, mT[:, c, tt * 128:(tt + 1) * 128], wb[:, c, :], start=(c == 0), stop=(c == 15)),
                        reads=[wk] + [("uT", 12 + c)], writes=[psk])
                R.dve(lambda e, ps=ps, tt=tt, cb=cb: e.tensor_tensor(
                    x1[:, tt, cb * 512:(cb + 1) * 512], x1[:, tt, cb * 512:(cb + 1) * 512], ps, ALU.add),
                    reads=[psk, ("x1", tt)], writes=[("x1", tt)])
        run_pipeline([norm_unit(None, x1[:, tt, :], ("x1", tt), 16, hTb, "hTb", tt * 128, pst4_rot, xs_rot, junk)
                      for tt in range(4)])
        for jg in range(11):
            wa, wak = load_w(WFI_d[2 * jg], 16, ("WFI", 2 * jg))
            wbb, wbk = load_w(WFI_d[2 * jg + 1], 16, ("WFI", 2 * jg + 1))
            for jj in range(4):
                j = jg * 4 + jj
                Pa, Pak = next_ps(psG_rot)
                for kc in range(16):
                    R.pe(lambda e, Pa=Pa, wa=wa, kc=kc, jj=jj: e.matmul(
                        Pa, wa[:, kc, jj * 128:(jj + 1) * 128], hTb[:, kc, :], start=(kc == 0), stop=(kc == 15)),
                        reads=[wak, "hTb"], writes=[Pak])
                Pb, Pbk = next_ps(psG_rot)
                for kc in range(16):
                    R.pe(lambda e, Pb=Pb, wbb=wbb, kc=kc, jj=jj: e.matmul(
                        Pb, wbb[:, kc, jj * 128:(jj + 1) * 128], hTb[:, kc, :], start=(kc == 0), stop=(kc == 15)),
                        reads=[wbk, "hTb"], writes=[Pbk])
                sg, sgk = sg_rot.next()
                R.act(lambda e, sg=sg, Pa=Pa: e.activation(sg, Pa, AF.Silu), reads=[Pak], writes=[sgk])
                R.dve(lambda e, sg=sg, Pb=Pb, j=j: e.tensor_tensor(uT[:, j, :], sg, Pb, ALU.mult),
                      reads=[sgk, Pbk], writes=[("uT", j)])
        if tb + 1 < 4:
            load_block_acts(t0 + 512)
        for cb in range(4):
            accs = [next_ps(psG_rot) for _ in range(4)]
            for kp in range(4):
                wb, wk = load_w(WFO_d[cb * 4 + kp], 11, ("WFO", cb * 4 + kp))
                for tt in range(4):
                    ps, psk = accs[tt]
                    for i in range(11):
                        j = kp * 11 + i
                        R.pe(lambda e, ps=ps, wb=wb, i=i, j=j, tt=tt, kp=kp: e.matmul(
                            ps, uT[:, j, tt * 128:(tt + 1) * 128], wb[:, i, :], start=(kp == 0 and i == 0), stop=(kp == 3 and i == 10)),
                            reads=[wk, ("uT", j)], writes=[psk])
            for tt in range(4):
                ps, psk = accs[tt]
                og, ogk = ostg_rot.next()
                R.dve(lambda e, ps=ps, og=og, tt=tt, cb=cb: e.tensor_tensor(og, x1[:, tt, cb * 512:(cb + 1) * 512], ps, ALU.add),
                      reads=[psk, ("x1", tt)], writes=[ogk])
                R.dma("sp", lambda e, og=og, tt=tt, cb=cb, t0=t0: e.dma_start(
                    out=out_d[t0 + tt * 128:t0 + (tt + 1) * 128, cb * 512:(cb + 1) * 512], in_=og), reads=[ogk], writes=["out"])

    with contextlib.ExitStack() as es:
        sems = {n: es.enter_context(nc.semaphore("s_" + n)) for n in ("pe", "act", "dve", "pool")}
        sems["dma"] = {"sp": [es.enter_context(nc.semaphore(f"dsp{i}")) for i in range(24)],
                       "pool": [es.enter_context(nc.semaphore(f"dpl{i}")) for i in range(12)],
                       "pool:cv": [es.enter_context(nc.semaphore(f"dcv{i}")) for i in range(6)]}
        block = es.enter_context(nc.Block())
        cc = R.prepare(sems)

        @block.tensor
        def _(e):
            R.run_engine("pe", e, sems)

        @block.scalar
        def _(e):
            R.run_engine("act", e, sems)

        @block.vector
        def _(e):
            R.run_engine("dve", e, sems)

        @block.gpsimd
        def _(e):
            R.run_engine("pool", e, sems)
            R.final_waits("pool", e, sems)

        @block.sync
        def _(e):
            R.run_engine("sp", e, sems)
            R.final_waits("sp", e, sems)
    build_program.R = R
    return nc, len(R.ops), cc


def _t5_bucket(rel):
    nb = 16
    ret = np.where(rel > 0, nb, 0)
    n = np.abs(rel)
    max_exact = 8
    large = max_exact + (np.log(np.maximum(n, 1).astype(np.float32) / max_exact)
                         / math.log(1024 / max_exact) * (nb - max_exact)).astype(np.int32)
    large = np.minimum(large, nb - 1)
    return ret + np.where(n < max_exact, n, large)


def _rope_table(pos):
    r = (pos // 64).astype(np.float32)
    c = (pos % 64).astype(np.float32)
    inv = (10000.0 ** (-np.arange(32, dtype=np.float32) / 32)).astype(np.float32)
    ang = np.concatenate([r[:, None] * inv, c[:, None] * inv], axis=-1).astype(np.float32)
    return np.concatenate([np.cos(ang), np.sin(ang)], axis=-1).astype(np.float32)


def _bias_tiles(rel_bias, parity):
    tiles = np.full((24, 128, 128), NEGB, dtype=np.float32)
    kk = np.arange(128)[:, None]
    qq = np.arange(128)[None, :]

    def tile(rel, d, head, kvalid=None):
        valid = np.abs(rel) <= 64
        if kvalid is not None:
            valid = valid & kvalid
        b = _t5_bucket(rel * d)
        return np.where(valid, rel_bias[b, head], np.float32(NEGB)).astype(np.float32)

    for g, d in enumerate((1, 4)):
        for h in range(2):
            head = 2 * g + h
            base = (g * 2 + h) * 5
            tiles[base + 0] = tile(kk - 128 - qq, d, head)
            tiles[base + 1] = tile(kk - qq, d, head)
            tiles[base + 2] = tile(kk + 128 - qq, d, head)
            if parity == 1:
                tiles[base + 3] = tiles[base + 0]
            if parity == 0:
                tiles[base + 4] = tiles[base + 2]
    for h in range(2):
        head = 4 + h
        base = 20 + h * 2
        tiles[base + 0] = tile(kk - qq, 16, head)
        k64 = (kk < 64)
        if parity == 0:
            rel = (128 + kk) - qq
        else:
            rel = (kk - 64) - qq
        t = tile(rel, 16, head, kvalid=k64)
        tiles[base + 1] = t
    return np.ascontiguousarray(tiles.transpose(1, 0, 2).reshape(128, 24 * 128))


_PROG = None


def kernel(x, mem, rel_bias, g_mix, w_in, g_qa, g_ka, g_qb, g_kb, g_mem, w_mem_kv, g_qc, g_kc,
           w_br_a, w_br_b, w_br_c, w_o, g_ffn, w_ffn_in, w_ffn_out):
    global _PROG
    if _PROG is None:
        _PROG = build_program()
    nc = _PROG[0]
    in_maps = make_in_maps(x, mem, rel_bias, g_mix, w_in, g_qa, g_ka, g_qb, g_kb, g_mem, w_mem_kv, g_qc, g_kc,
                           w_br_a, w_br_b, w_br_c, w_o, g_ffn, w_ffn_in, w_ffn_out)
    res = run_bass_kernel_spmd(nc, in_maps, core_ids=list(range(8)))
    out = np.empty((4, SEQ, D), dtype=np.float32)
    for core in range(8):
        b, par = core // 2, core % 2
        out[b, par * 2048:(par + 1) * 2048] = res.results[core]["out"]
    return out


def make_in_maps(x, mem, rel_bias, g_mix, w_in, g_qa, g_ka, g_qb, g_kb, g_mem, w_mem_kv, g_qc, g_kc,
                 w_br_a, w_br_b, w_br_c, w_o, g_ffn, w_ffn_in, w_ffn_out):
    f = lambda a: np.ascontiguousarray(np.asarray(a, dtype=np.float32))
    x = f(x)
    mem = f(mem)
    rel_bias = f(rel_bias)
    col = lambda g: f(g)[0].reshape(16, 128).T
    gcol = np.ascontiguousarray(np.concatenate([col(g_mix), col(g_ffn), col(g_mem)], axis=1))
    ghead = np.ascontiguousarray(np.broadcast_to(
        np.concatenate([f(g)[0] for g in (g_qa, g_ka, g_qb, g_kb, g_qc, g_kc)])[None, :], (128, 768)))
    ident = np.eye(128, dtype=np.float32)
    shared = {
        "w_in": f(w_in)[0], "w_mem": f(w_mem_kv)[0], "w_br_a": f(w_br_a)[0], "w_br_b": f(w_br_b)[0],
        "w_br_c": f(w_br_c)[0], "w_o": f(w_o)[0], "w_fi": f(w_ffn_in)[0], "w_fo": f(w_ffn_out)[0],
        "gcol": gcol, "ghead": ghead, "ident": ident,
    }
    in_maps = []
    for core in range(8):
        b, par = core // 2, core % 2
        if par == 0:
            own = x[b, 0:2048]
            oth = x[b, 2048:4096]
            pos_own = np.arange(0, 2048)
            pos_oth = np.arange(2048, 4096)
        else:
            own = x[b, 2048:4096]
            oth = np.concatenate([x[b, 1024:2048], x[b, 0:1024]], axis=0)
            pos_own = np.arange(2048, 4096)
            pos_oth = np.concatenate([np.arange(1024, 2048), np.arange(0, 1024)])
        m = dict(shared)
        m["x_own"] = np.ascontiguousarray(own)
        m["x_oth"] = np.ascontiguousarray(oth)
        m["memx"] = np.ascontiguousarray(mem[b])
        m["rope_own"] = np.ascontiguousarray(_rope_table(pos_own).reshape(16, 128, 128).transpose(1, 0, 2))
        m["rope_oth"] = np.ascontiguousarray(_rope_table(pos_oth).reshape(16, 128, 128).transpose(1, 0, 2))
        m["abias"] = _bias_tiles(rel_bias, par)
        in_maps.append(m)
    return in_maps
```

```python
import contextlib
import math
import numpy as np
import concourse.bass as bass
import concourse.mybir as mybir
from concourse.bass_utils import run_bass_kernel_spmd

F32 = mybir.dt.float32
BF16 = mybir.dt.bfloat16
U8 = mybir.dt.uint8
AF = mybir.ActivationFunctionType
ALU = mybir.AluOpType
AX = mybir.AxisListType

D = 2048
SEQ = 4096
NOWN = 2048
DFF = 5632
EPS = 1e-6
SCALE = 1.0 / math.sqrt(128.0)
NEGB = -30000.0
DILS = (1, 4, 16)
ENG_NAMES = ("pe", "act", "dve", "pool", "sp")


class Op:
    __slots__ = ("eng", "fn", "reads", "writes", "dma", "deps", "need_inc", "cnt", "idx", "bar", "spool")

    def __init__(self, eng, fn, reads, writes, dma):
        self.eng = eng
        self.fn = fn
        self.reads = reads
        self.writes = writes
        self.dma = dma
        self.deps = ()
        self.need_inc = False
        self.cnt = 0
        self.bar = False
        self.spool = eng


class Rec:
    def __init__(self):
        self.ops = []

    disabled = False

    def op(self, eng, fn, reads=(), writes=(), dma=False):
        if self.disabled:
            return None
        o = Op(eng, fn, tuple(reads), tuple(writes), dma)
        o.idx = len(self.ops)
        self.ops.append(o)
        return o

    def pe(self, fn, reads=(), writes=()):
        return self.op("pe", fn, reads, writes)

    def act(self, fn, reads=(), writes=()):
        return self.op("act", fn, reads, writes)

    def dve(self, fn, reads=(), writes=()):
        return self.op("dve", fn, reads, writes)

    def pool(self, fn, reads=(), writes=()):
        return self.op("pool", fn, reads, writes)

    def dma(self, eng, fn, reads=(), writes=(), spool=None):
        o = self.op(eng, fn, reads, writes, dma=True)
        if o is not None and spool is not None:
            o.spool = spool
        return o

    def barrier(self):
        if self.disabled:
            return
        for e in ENG_NAMES:
            o = self.op(e, None)
            o.bar = True

    def analyze(self):
        last_w = {}
        readers = {}
        last_on = {}
        dmas_since = []
        i = 0
        n = len(self.ops)
        while i < n:
            o = self.ops[i]
            if o.bar:
                grp = []
                while i < n and self.ops[i].bar:
                    grp.append(self.ops[i])
                    i += 1
                forced = [v for v in last_on.values()] + list(dmas_since)
                for b in grp:
                    b.deps = tuple(sorted(forced))
                for j in forced:
                    self.ops[j].need_inc = True
                last_w = {}
                readers = {}
                dmas_since = []
                continue
            deps = set()
            for k in o.reads:
                w = last_w.get(k)
                if w is not None:
                    deps.add(w)
            for k in o.writes:
                w = last_w.get(k)
                if w is not None:
                    deps.add(w)
                for r in readers.get(k, ()):
                    deps.add(r)
            deps.discard(o.idx)
            keep = []
            for j in deps:
                p = self.ops[j]
                if (not p.dma) and (not o.dma) and p.eng == o.eng:
                    if o.eng == "pe":
                        continue
                keep.append(j)
            best = {}
            kept = []
            for j in keep:
                p = self.ops[j]
                if p.dma:
                    kept.append(j)
                elif j > best.get(p.eng, -1):
                    best[p.eng] = j
            o.deps = tuple(sorted(kept + list(best.values())))
            for j in o.deps:
                self.ops[j].need_inc = True
            for k in o.reads:
                readers.setdefault(k, []).append(o.idx)
            for k in o.writes:
                last_w[k] = o.idx
                readers[k] = []
            if o.dma:
                dmas_since.append(o.idx)
            else:
                last_on[o.eng] = o.idx
            i += 1

    def prepare(self, sems):
        self.analyze()
        ccount = {e: 0 for e in ENG_NAMES}
        dma_sems = sems["dma"]
        self.dma_cnt = {e: [0] * len(dma_sems[e]) for e in dma_sems}
        rr = {e: 0 for e in dma_sems}
        for o in self.ops:
            if o.bar:
                continue
            if o.dma:
                nse = len(dma_sems[o.spool])
                s = rr[o.spool] % nse
                rr[o.spool] += 1
                self.dma_cnt[o.spool][s] += 16
                o.cnt = (o.spool, s, self.dma_cnt[o.spool][s])
            elif o.need_inc:
                ccount[o.eng] += 1
                o.cnt = ccount[o.eng]
        self.streams = {e: [] for e in ENG_NAMES}
        for o in self.ops:
            self.streams[o.eng].append(o)
        return ccount

    def run_engine(self, ename, eng, sems):
        waited = {}
        dma_sems = sems["dma"]
        for o in self.streams[ename]:
            need = {}
            for j in o.deps:
                p = self.ops[j]
                if p.dma:
                    pe_, s, v = p.cnt
                    key = ("d", pe_, s)
                else:
                    key = p.eng
                    v = p.cnt
                if v > need.get(key, 0):
                    need[key] = v
            if o.dma:
                sp_, s, v = o.cnt
                key = ("d", sp_, s)
                if v > 16 and need.get(key, 0) < v - 16:
                    need[key] = v - 16
            for key, v in need.items():
                if waited.get(key, 0) >= v:
                    continue
                if isinstance(key, tuple):
                    eng.wait_ge(dma_sems[key[1]][key[2]], v)
                else:
                    eng.wait_ge(sems[key], v)
                waited[key] = v
            if o.bar:
                continue
            ins = o.fn(eng)
            if o.dma:
                sp_, s, v = o.cnt
                ins.then_inc(dma_sems[sp_][s], 16)
            elif o.need_inc:
                ins.then_inc(sems[ename], 1)

    def final_waits(self, ename, eng, sems):
        for pool_name, cnts in self.dma_cnt.items():
            if pool_name.split(":")[0] != ename:
                continue
            for s, v in enumerate(cnts):
                if v > 0:
                    eng.wait_ge(sems["dma"][pool_name][s], v)


class Rot:
    def __init__(self, name, aps, track=False):
        self.name = name
        self.aps = aps
        self.i = 0
        self.track = track
        self.held = [False] * len(aps)

    def next(self):
        n = len(self.aps)
        if not self.track:
            i = self.i % n
            self.i += 1
            return self.aps[i], (self.name, i)
        for d in range(n):
            i = (self.i + d) % n
            if not self.held[i]:
                self.held[i] = True
                self.i = i + 1
                return self.aps[i], (self.name, i)
        raise RuntimeError(f"rotation {self.name} exhausted ({n} buffers)")

    def release(self, key):
        self.held[key[1]] = False


def build_program(stop_after=None, rope_eng="pool", dbg=None):
    nc = bass.Bass("TRN2", target_bir_lowering=False)
    R = Rec()
    dbg = dbg or {}

    def din(name, shape, dt=F32):
        return nc.dram_tensor(name, list(shape), dt, kind="ExternalInput").ap()

    x_own = din("x_own", [NOWN, D])
    x_oth = din("x_oth", [NOWN, D])
    memx = din("memx", [256, D])
    w_in = din("w_in", [D, 10240])
    w_mem = din("w_mem", [D, 1024])
    w_br = [din("w_br_a", [256, D]), din("w_br_b", [768, D]), din("w_br_c", [512, D])]
    w_o = din("w_o", [D, D])
    w_fi = din("w_fi", [D, 2 * DFF])
    w_fo = din("w_fo", [DFF, D])
    gcol_d = din("gcol", [128, 48])
    ghead_d = din("ghead", [128, 768])
    rope_own_d = din("rope_own", [128, 16, 128])
    rope_oth_d = din("rope_oth", [128, 16, 128])
    abias_d = din("abias", [128, 24 * 128])
    ident_d = din("ident", [128, 128])
    out_d = nc.dram_tensor("out", [NOWN, D], F32, kind="ExternalOutput").ap()

    def scr(name, shape):
        return nc.dram_tensor(name, list(shape), BF16).ap()

    hT_own_d = scr("hT_own", [16, 128, NOWN])
    QA_d = [scr(f"QA{g}", [2, 128, 2048]) for g in range(3)]
    KA_d = [scr(f"KA{g}", [2, 128, 3072]) for g in range(3)]
    VA_d = [scr(f"VA{g}", [3072, 256]) for g in range(3)]
    QB_d = scr("QB", [6, 128, 2048])
    KB_d = scr("KB", [2, 128, 4096])
    VB_d = scr("VB", [4096, 256])
    QC_d = scr("QC", [4, 128, 2048])
    KC_d = scr("KC", [4, 128, 256])
    VC_d = scr("VC", [256, 512])
    OT_d = scr("OT", [12, 128, NOWN])
    WG_d = scr("WG", [12, 128, 16 * 512])
    WBR_d = scr("WBR", [12, 128, 6 * 512])
    WO_d = scr("WO", [4, 128, 16 * 512])
    WFI_d = scr("WFI", [22, 128, 16 * 512])
    WFO_d = scr("WFO", [16, 128, 11 * 512])

    ARENA = 206 * 1024
    arena = nc.alloc_sbuf_tensor("arena", [128, ARENA], U8).ap()

    class Carver:
        def __init__(self, base):
            self.off = base

        def get(self, shape, dt):
            esz = 4 if dt == F32 else 2
            nb = int(np.prod(shape[1:])) * esz
            nb_al = (nb + 63) // 64 * 64
            assert self.off + nb_al <= ARENA, (self.off, nb_al)
            ap = arena[:, self.off:self.off + nb].bitcast(dt)
            self.off += nb_al
            if len(shape) == 3:
                ap = ap.rearrange("p (a b) -> p a b", a=shape[1])
            return ap

    cv = Carver(0)
    ident = cv.get([128, 128], BF16)
    ones = cv.get([128, 128], BF16)
    gcol = cv.get([128, 48], F32)
    ghead = cv.get([128, 768], F32)
    stat_rot = Rot("st", [cv.get([128, 16], F32) for _ in range(16)], track=True)
    PERS = cv.off

    psb = [nc.alloc_psum_tensor(f"ps{i}", [128, 512], F32).ap() for i in range(8)]

    br_rows = [(0, 2), (2, 6), (8, 4)]
    conv_list = []
    conv_last = {}

    def add_conv(dst, k, src, key):
        for k0 in range(0, k, 4):
            kn = min(4, k - k0)
            conv_last[key] = len(conv_list)
            conv_list.append(lambda k0=k0, kn=kn, pace=(): R.dma("pool", lambda e: e.dma_start(
                out=dst[:, k0 * 512:(k0 + kn) * 512].rearrange("p (k c) -> p k c", k=kn),
                in_=src[k0 * 128:(k0 + kn) * 128, :].rearrange("(k p) c -> p k c", p=128)),
                reads=list(pace), writes=[(key, k0)], spool="pool:cv"))

    for cq in range(4):
        for bi in range(3):
            add_conv(WG_d[bi * 4 + cq], 16, w_in[:, 4096 + bi * 2048 + cq * 512:4096 + bi * 2048 + (cq + 1) * 512], ("WG", bi * 4 + cq))
            add_conv(WBR_d[bi * 4 + cq], br_rows[bi][1], w_br[bi][:, cq * 512:(cq + 1) * 512], ("WBR", bi * 4 + cq))
    for cb in range(4):
        add_conv(WO_d[cb], 16, w_o[:, cb * 512:(cb + 1) * 512], ("WO", cb))
    for jg in range(11):
        add_conv(WFI_d[2 * jg], 16, w_fi[:, jg * 512:(jg + 1) * 512], ("WFI", 2 * jg))
        add_conv(WFI_d[2 * jg + 1], 16, w_fi[:, DFF + jg * 512:DFF + (jg + 1) * 512], ("WFI", 2 * jg + 1))
    for cb in range(4):
        for kp in range(4):
            add_conv(WFO_d[cb * 4 + kp], 11, w_fo[kp * 1408:(kp + 1) * 1408, cb * 512:(cb + 1) * 512], ("WFO", cb * 4 + kp))
    conv_pos = [0]

    def pump_conv(n, pace=()):
        while n > 0 and conv_pos[0] < len(conv_list):
            conv_list[conv_pos[0]](pace=pace)
            conv_pos[0] += 1
            n -= 1

    def ensure_conv(key):
        while conv_pos[0] <= conv_last[key]:
            pump_conv(1)

    pump_tick = [0]

    def pump_every(n, pace=()):
        pump_tick[0] += 1
        if pump_tick[0] % n == 0:
            pump_conv(1, pace)

    R.dma("pool", lambda e: e.dma_start(out=ident, in_=ident_d), writes=["ident"])
    R.dve(lambda e: e.memset(ones, 1.0), writes=["ones"])
    R.dma("sp", lambda e: e.dma_start(out=gcol, in_=gcol_d), writes=["gcol"])
    R.dma("sp", lambda e: e.dma_start(out=ghead, in_=ghead_d), writes=["ghead"])

    def run_pipeline(units):
        active = []
        it = iter(units)
        pending = True
        while pending or active:
            nxt = []
            for gen in active:
                try:
                    next(gen)
                    nxt.append(gen)
                except StopIteration:
                    pass
            active = nxt
            if pending:
                try:
                    u = next(it)
                    try:
                        next(u)
                        active.append(u)
                    except StopIteration:
                        pass
                except StopIteration:
                    pending = False

    cv = Carver(PERS)
    rope_own = cv.get([128, 16, 128], F32)
    rope_oth = cv.get([128, 16, 128], F32)
    hT_bufs = [cv.get([128, 16, 1024], BF16) for _ in range(2)]
    xin_rot = Rot("xin", [cv.get([128, D], F32) for _ in range(2)], track=True)
    xs_rot = Rot("xs", [cv.get([128, D], BF16) for _ in range(2)], track=True)
    junk = cv.get([128, D], BF16)
    wblk_rot = Rot("wblk", [cv.get([128, 16, 512], BF16) for _ in range(3)])
    qf_rot = Rot("qf", [cv.get([128, 512], F32) for _ in range(10)], track=True)
    rt_rot = Rot("rt", [cv.get([128, 1024], F32) for _ in range(2)])
    qb_rot = Rot("qb16", [cv.get([128, 512], BF16) for _ in range(8)], track=True)
    sqb_rot = Rot("sqb", [cv.get([128, 512], BF16) for _ in range(3)], track=True)
    stg_rot = Rot("stg", [cv.get([128, 512], BF16) for _ in range(2)])
    vstg_rot = Rot("vstg", [cv.get([128, 512], BF16) for _ in range(2)])
    G1_END = cv.off

    R.dma("sp", lambda e: e.dma_start(out=rope_own, in_=rope_own_d), writes=["rope_own"])
    R.dma("sp", lambda e: e.dma_start(out=rope_oth, in_=rope_oth_d), writes=["rope_oth"])

    psA_rot = Rot("psA", [(i, psb[i]) for i in range(0, 5)], track=True)
    pst_rot = Rot("pst", [(i, psb[i].bitcast(BF16)) for i in (5, 6, 7)])

    def next_ps(rot):
        (i, ap), _ = rot.next()
        return ap, ("ps", i)

    def next_ps_t(rot):
        (i, ap), rk = rot.next()
        return ap, ("ps", i), rk

    def rstd_stage_ms(st, stk, ntok, n, inv_n, c_ss, tmp):
        R.dve(lambda e: e.tensor_scalar(st[:ntok, tmp:tmp + n], st[:ntok, c_ss:c_ss + n], inv_n, EPS, ALU.mult, ALU.add),
              reads=[stk], writes=[stk])

    def rstd_stage_lnexp(st, stk, ntok, n, tmp, c_out):
        R.act(lambda e: e.activation(st[:ntok, tmp:tmp + n], st[:ntok, tmp:tmp + n], AF.Ln), reads=[stk], writes=[stk])
        R.act(lambda e: e.activation(st[:ntok, c_out:c_out + n], st[:ntok, tmp:tmp + n], AF.Exp, scale=-0.5),
              reads=[stk], writes=[stk])

    def norm_unit(load_fn, x_ap, x_key, goff, hT, hT_key, col0, pr, xsr, jk):
        loaded = load_fn is not None
        if loaded:
            x_ap, x_key = load_fn()
            yield
        st, stk = stat_rot.next()
        R.act(lambda e: e.activation(jk, x_ap, AF.Square, accum_out=st[:, 0:1]), reads=[x_key], writes=["junk", stk])
        yield
        rstd_stage_ms(st, stk, 128, 1, 1.0 / D, 0, 5)
        yield
        rstd_stage_lnexp(st, stk, 128, 1, 5, 1)
        xs, xsk = xsr.next()
        R.act(lambda e: e.activation(xs, x_ap, AF.Copy, scale=st[:, 1:2]), reads=[x_key, stk], writes=[xsk])
        stat_rot.release(stk)
        if loaded:
            xin_rot.release(x_key)
        yield
        for half in range(2):
            pt, ptk = next_ps(pr)
            for q in range(8):
                kc = half * 8 + q
                R.pe(lambda e, pt=pt, q=q, kc=kc: e.transpose(pt[:, q * 128:(q + 1) * 128], xs[:, kc * 128:(kc + 1) * 128], ident),
                     reads=[xsk, "ident"], writes=[ptk])
            R.dve(lambda e, pt=pt, half=half: e.tensor_tensor(
                hT[:, half * 8:half * 8 + 8, col0:col0 + 128], pt.rearrange("p (a b) -> p a b", a=8),
                gcol[:, goff + half * 8:goff + half * 8 + 8].unsqueeze(2).to_broadcast([128, 8, 128]), ALU.mult),
                reads=[ptk, "gcol"], writes=[hT_key])
        xsr.release(xsk)

    def load_unit(job):
        wb, wk = wblk_rot.next()
        off = 0
        for wap in job["wsegs"]:
            ncols = wap.shape[1]
            R.dma("pool", lambda e, wap=wap, off=off, ncols=ncols: e.dma_start(
                out=wb[:, :, off:off + ncols], in_=wap.rearrange("(k p) c -> p k c", p=128)), writes=[wk])
            off += ncols
        job["wb"], job["wk"], job["W"] = wb, wk, off
        return
        yield

    def proj_unit(job, hT, hT_key, csl, ntok, info):
        wb, wk, W = job["wb"], job["wk"], job["W"]
        jk = junk
        ps, psk, psrk = next_ps_t(psA_rot)
        for kc in range(16):
            R.pe(lambda e, kc=kc: e.matmul(ps[:ntok, :W], hT[:, kc, csl], wb[:, kc, :W], start=(kc == 0), stop=(kc == 15)),
                 reads=[hT_key, wk], writes=[psk])
        pump_every(3)
        yield
        qs = []
        for (coff, ncols, kind, gi, rope_fn, dest_fn) in job["segs"]:
            if kind == "v":
                vs, vsk = vstg_rot.next()
                R.act(lambda e, vs=vs, coff=coff, ncols=ncols: e.copy(vs[:ntok, :ncols], ps[:ntok, coff:coff + ncols]),
                      reads=[psk], writes=[vsk])
                dram_ap, dkey = dest_fn(info, ntok)
                R.dma("sp", lambda e, vs=vs, dram_ap=dram_ap, ncols=ncols: e.dma_start(out=dram_ap, in_=vs[:ntok, :ncols]),
                      reads=[vsk], writes=[dkey])
            else:
                nh = ncols // 128
                qf, qfk = qf_rot.next()
                R.act(lambda e, qf=qf, coff=coff, ncols=ncols: e.copy(qf[:ntok, :ncols], ps[:ntok, coff:coff + ncols]),
                      reads=[psk], writes=[qfk])
                qs.append([coff, nh, gi, rope_fn, dest_fn, None, None, qf, qfk])
        psA_rot.release(psrk)
        if not qs:
            return
        yield
        for q in qs:
            coff, nh, gi, rope_fn, dest_fn, _, _, qf, qfk = q
            st, stk = stat_rot.next()
            q[5], q[6] = st, stk
            if rope_fn is not None:
                for h in range(nh):
                    R.act(lambda e, st=st, h=h, qf=qf: e.activation(
                        jk[:ntok, h * 128:(h + 1) * 128], qf[:ntok, h * 128:(h + 1) * 128], AF.Square,
                        accum_out=st[:ntok, h:h + 1]), reads=[qfk], writes=["junk", stk])
                q.append(None)
            else:
                sqb, sqk = sqb_rot.next()
                R.pool(lambda e, sqb=sqb, qf=qf, nh=nh: e.tensor_tensor(
                    sqb[:ntok, :nh * 128], qf[:ntok, :nh * 128], qf[:ntok, :nh * 128], ALU.mult), reads=[qfk], writes=[sqk])
                q.append((sqb, sqk))
        yield
        for q in qs:
            coff, nh, gi, rope_fn, dest_fn, st, stk, qf, qfk, sq = q
            if sq is not None:
                sqb, sqk = sq
                R.dve(lambda e, st=st, sqb=sqb, nh=nh: e.tensor_reduce(
                    st[:ntok, 0:nh], sqb[:ntok, :nh * 128].rearrange("p (h d) -> p h d", h=nh), AX.X, ALU.add),
                    reads=[sqk], writes=[stk])
                sqb_rot.release(sqk)
        qs = [tuple(q[:9]) for q in qs]
        for (coff, nh, gi, rope_fn, dest_fn, st, stk, qf, qfk) in qs:
            rstd_stage_ms(st, stk, ntok, nh, 1.0 / 128, 0, 12)
        yield
        for (coff, nh, gi, rope_fn, dest_fn, st, stk, qf, qfk) in qs:
            rstd_stage_lnexp(st, stk, ntok, nh, 12, 8)
        yield
        outs = []
        for (coff, nh, gi, rope_fn, dest_fn, st, stk, qf, qfk) in qs:
            qb, qbk = qb_rot.next()
            rope = rope_fn(info) if rope_fn is not None else None
            if rope is None:
                dst, dkeys = qb, [(qbk, 0), (qbk, 1)]
            else:
                dst, dkeys = qf, [qfk]
            for h in range(nh):
                R.dve(lambda e, st=st, h=h, qf=qf, dst=dst, gi=gi: e.scalar_tensor_tensor(
                    dst[:ntok, h * 128:(h + 1) * 128], qf[:ntok, h * 128:(h + 1) * 128], st[:ntok, 8 + h:9 + h],
                    ghead[:ntok, gi * 128:(gi + 1) * 128], ALU.mult, ALU.mult),
                    reads=[qfk, stk, "ghead"], writes=dkeys)
            stat_rot.release(stk)
            if rope is None:
                qf_rot.release(qfk)
            outs.append((nh, dest_fn, qb, qbk, qf, qfk, rope))
        yield
        any_rope = False
        for (nh, dest_fn, qb, qbk, qf, qfk, rope) in outs:
            if rope is None:
                continue
            any_rope = True
            rp, rpk = rope
            W2 = nh * 128
            rt, rtk = rt_rot.next()
            qf3 = qf[:ntok, :W2].rearrange("p (h d) -> p h d", h=nh)
            qb3 = qb[:ntok, :W2].rearrange("p (h d) -> p h d", h=nh)
            x0 = qf3[:, :, 0::2]
            x1 = qf3[:, :, 1::2]
            cs = rp[:ntok, 0:64].unsqueeze(1).to_broadcast([ntok, nh, 64])
            sn = rp[:ntok, 64:128].unsqueeze(1).to_broadcast([ntok, nh, 64])
            t = rt[:ntok, :2 * W2].rearrange("p (a h d) -> p a h d", a=4, h=nh)
            R.op(rope_eng, lambda e, t=t, x0=x0, cs=cs: e.tensor_tensor(t[:, 0], x0, cs, ALU.mult), reads=[qfk, rpk], writes=[(rtk, 0)])
            R.op(rope_eng, lambda e, t=t, x1=x1, sn=sn: e.tensor_tensor(t[:, 1], x1, sn, ALU.mult), reads=[qfk, rpk], writes=[(rtk, 1)])
            R.op(rope_eng, lambda e, t=t, x0=x0, sn=sn: e.tensor_tensor(t[:, 2], x0, sn, ALU.mult), reads=[qfk, rpk], writes=[(rtk, 2)])
            R.op(rope_eng, lambda e, t=t, x1=x1, cs=cs: e.tensor_tensor(t[:, 3], x1, cs, ALU.mult), reads=[qfk, rpk], writes=[(rtk, 3)])
            R.op(rope_eng, lambda e, t=t, qb3=qb3: e.tensor_tensor(qb3[:, :, 0::2], t[:, 0], t[:, 1], ALU.subtract),
                   reads=[(rtk, 0), (rtk, 1)], writes=[(qbk, 0)])
            R.op(rope_eng, lambda e, t=t, qb3=qb3: e.tensor_tensor(qb3[:, :, 1::2], t[:, 2], t[:, 3], ALU.add),
                   reads=[(rtk, 2), (rtk, 3)], writes=[(qbk, 1)])
            qf_rot.release(qfk)
        if any_rope:
            yield
        for (nh, dest_fn, qb, qbk, qf, qfk, rope) in outs:
            pt, ptk = next_ps(pst_rot)
            for h in range(nh):
                R.pe(lambda e, h=h, pt=pt, qb=qb: e.transpose(pt[:, h * ntok:(h + 1) * ntok], qb[:ntok, h * 128:(h + 1) * 128],
                                                              ident[:ntok, :ntok]),
                     reads=[(qbk, 0), (qbk, 1), "ident"], writes=[ptk])
            qb_rot.release(qbk)
            sg, sgk = stg_rot.next()
            R.act(lambda e, sg=sg, pt=pt, nh=nh: e.copy(sg[:, :nh * ntok], pt[:, :nh * ntok]), reads=[ptk], writes=[sgk])
            dram_ap, dkey = dest_fn(info, ntok)
            R.dma("sp", lambda e, sg=sg, dram_ap=dram_ap, nh=nh: e.dma_start(
                out=dram_ap.rearrange("h p t -> p h t"), in_=sg[:, :nh * ntok].rearrange("p (h t) -> p h t", h=nh)),
                reads=[sgk], writes=[dkey])

    def a_tiles(chunk_kind, k):
        res = {}
        for g, d in enumerate(DILS):
            Lo, Lh = NOWN // d, 1024 // d
            Lr = Lo + Lh
            per = 1024 // d
            ntok = min(128, per)
            tl = []
            for cl in range(d):
                for mt in range(per // ntok):
                    start = cl + d * mt * ntok
                    stop = min(start + d * ntok, 1024)
                    csl = slice(start, stop, d) if d > 1 else slice(start, start + ntok)
                    if chunk_kind == "own":
                        m0 = k * per + mt * ntok
                        tl.append((csl, ntok, (cl * Lo + m0, cl * Lr + m0)))
                    else:
                        tl.append((csl, ntok, (None, cl * Lr + Lo + mt * ntok)))
            res[g] = tl
        return res

    def wcols(w, c0, n):
        return w[:, c0:c0 + n]

    nat_tiles = [(slice(i * 128, (i + 1) * 128), 128, i) for i in range(8)]

    def chunk_jobs(kind, k):
        jobs = []
        if kind == "mem":
            jobs.append(dict(wsegs=[wcols(w_mem, 0, 512)], tiles=nat_tiles[:2],
                             segs=[(0, 512, "qk", 5, None, lambda i, n: (KC_d[:, :, i * 128:(i + 1) * 128], "KC_d"))]))
            jobs.append(dict(wsegs=[wcols(w_mem, 512, 512)], tiles=nat_tiles[:2],
                             segs=[(0, 512, "v", 0, None, lambda i, n: (VC_d[i * 128:(i + 1) * 128, :], "VC_d"))]))
            return jobs
        if kind == "own":
            base = k * 1024
            rp = lambda i: (rope_own[:, k * 8 + i, :], "rope_own")
            cols = lambda i: slice(base + i * 128, base + (i + 1) * 128)
            jobs.append(dict(wsegs=[wcols(w_in, 2304, 512)], tiles=nat_tiles,
                             segs=[(0, 512, "qk", 2, rp, lambda i, n: (QB_d[0:4, :, cols(i)], "QB_d"))]))
            jobs.append(dict(wsegs=[wcols(w_in, 2816, 512)], tiles=nat_tiles,
                             segs=[(0, 256, "qk", 2, rp, lambda i, n: (QB_d[4:6, :, cols(i)], "QB_d")),
                                   (256, 256, "qk", 3, rp, lambda i, n: (KB_d[0:2, :, cols(i)], "KB_d"))]))
            jobs.append(dict(wsegs=[wcols(w_in, 3328, 512)], tiles=nat_tiles,
                             segs=[(0, 256, "v", 0, None, lambda i, n: (VB_d[cols(i), :], "VB_d")),
                                   (256, 256, "qk", 4, None, lambda i, n: (QC_d[0:2, :, cols(i)], "QC_d"))]))
            jobs.append(dict(wsegs=[wcols(w_in, 3840, 256)], tiles=nat_tiles,
                             segs=[(0, 256, "qk", 4, None, lambda i, n: (QC_d[2:4, :, cols(i)], "QC_d"))]))
            at = a_tiles("own", k)
            for g in range(3):
                jobs.append(dict(wsegs=[wcols(w_in, g * 256, 256), wcols(w_in, 768 + g * 256, 256)], tiles=at[g],
                                 segs=[(0, 256, "qk", 0, None, lambda inf, n, g=g: (QA_d[g][:, :, inf[0]:inf[0] + n], ("QA_d", g))),
                                       (256, 256, "qk", 1, None, lambda inf, n, g=g: (KA_d[g][:, :, inf[1]:inf[1] + n], ("KA_d", g)))]))
                jobs.append(dict(wsegs=[wcols(w_in, 1536 + g * 256, 256)], tiles=at[g],
                                 segs=[(0, 256, "v", 0, None, lambda inf, n, g=g: (VA_d[g][inf[1]:inf[1] + n, :], ("VA_d", g)))]))
            return jobs
        base = 2048 + k * 1024
        rp = lambda i: (rope_oth[:, k * 8 + i, :], "rope_oth")
        cols = lambda i: slice(base + i * 128, base + (i + 1) * 128)
        jobs.append(dict(wsegs=[wcols(w_in, 3072, 512)], tiles=nat_tiles,
                         segs=[(0, 256, "qk", 3, rp, lambda i, n: (KB_d[0:2, :, cols(i)], "KB_d")),
                               (256, 256, "v", 0, None, lambda i, n: (VB_d[cols(i), :], "VB_d"))]))
        if k == 0:
            at = a_tiles("halo", 0)
            for g in range(3):
                jobs.append(dict(wsegs=[wcols(w_in, 768 + g * 256, 256), wcols(w_in, 1536 + g * 256, 256)], tiles=at[g],
                                 segs=[(0, 256, "qk", 1, None, lambda inf, n, g=g: (KA_d[g][:, :, inf[1]:inf[1] + n], ("KA_d", g))),
                                       (256, 256, "v", 0, None, lambda inf, n, g=g: (VA_d[g][inf[1]:inf[1] + n, :], ("VA_d", g)))]))
        return jobs

    def s1_units(kind, k, hT, hT_key):
        us = []
        ntiles = 2 if kind == "mem" else 8
        for t in range(ntiles):
            if kind == "mem":
                src, goff = memx[t * 128:(t + 1) * 128, :], 32
            elif kind == "own":
                src, goff = x_own[k * 1024 + t * 128:k * 1024 + (t + 1) * 128, :], 0
            else:
                src, goff = x_oth[k * 1024 + t * 128:k * 1024 + (t + 1) * 128, :], 0

            def load_fn(src=src):
                xin, xk = xin_rot.next()
                R.dma("sp", lambda e: e.dma_start(out=xin, in_=src), writes=[xk])
                return xin, xk
            gen = norm_unit(load_fn, None, None, goff, hT, hT_key, t * 128, pst_rot, xs_rot, junk)
            if kind == "own" and k == 1 and t == 5 and "cut" in dbg:
                import itertools
                gen = itertools.islice(gen, dbg["cut"])
            us.append(gen)
        return us

    def nop_unit():
        return
        yield

    def store_hT_unit(k, hT, hT_key):
        for _ in range(8):
            yield
        R.dma("sp", lambda e: e.dma_start(out=hT_own_d[:, :, k * 1024:(k + 1) * 1024].rearrange("k p t -> p k t"), in_=hT),
              reads=[hT_key], writes=["hT_own_d"])
        return
        yield

    chunks = [("mem", 0), ("oth", 0), ("oth", 1), ("own", 0), ("own", 1)]
    all_jobs = []
    for ci, (kind, k) in enumerate(chunks):
        for job in chunk_jobs(kind, k):
            job["ci"] = ci
            all_jobs.append(job)
    units = []
    LBL = {}
    units += s1_units("mem", 0, hT_bufs[0], ("hT", 0))
    units.append(load_unit(all_jobs[0]))
    ji = 0
    for ci, (kind, k) in enumerate(chunks):
        hT, hT_key = hT_bufs[ci % 2], ("hT", ci % 2)
        prim = []
        while ji < len(all_jobs) and all_jobs[ji]["ci"] == ci:
            job = all_jobs[ji]
            if ji + 1 < len(all_jobs):
                prim.append(load_unit(all_jobs[ji + 1])); LBL[id(prim[-1])] = ('load', ji + 1)
            for (csl, ntok, info) in job["tiles"]:
                prim.append(proj_unit(job, hT, hT_key, csl, ntok, info)); LBL[id(prim[-1])] = ('proj', ji, str(csl), ntok, str(info))
            ji += 1
        sec = []
        if ci + 1 < len(chunks):
            nk, nkk = chunks[ci + 1]
            sec = s1_units(nk, nkk, hT_bufs[(ci + 1) % 2], ("hT", (ci + 1) % 2))
            if nk == "own":
                sec += [None] * 8 + [store_hT_unit(nkk, hT_bufs[(ci + 1) % 2], ("hT", (ci + 1) % 2))]
        merged = []
        np_, ns_ = len(prim), len(sec)
        si = 0
        for pi, u in enumerate(prim):
            merged.append(u)
            while si < ns_ and (si + 1) * np_ <= (pi + 1) * (ns_ + 1) * 0.8 + 1e-9 and si < ns_:
                if sec[si] is not None:
                    merged.append(sec[si])
                si += 1
                break
        while si < ns_:
            if sec[si] is not None:
                merged.append(sec[si])
                merged.append(nop_unit())
            si += 1
        units += merged
    if dbg.get('print'):
        for i, u in enumerate(units):
            print(i, LBL.get(id(u), 's1/store'))
    run_pipeline([u for i, u in enumerate(units[:dbg.get('max_units', len(units))]) if i not in dbg.get('skip', ())])

    R.barrier()
    if stop_after == 1:
        R.disabled = True

    cv = Carver(PERS)
    abias = cv.get([128, 24 * 128], F32)
    Qa = cv.get([128, 2, 2048], BF16)
    Ka = cv.get([128, 2, 3072], BF16)
    Va = cv.get([128, 32, 256], BF16)
    Uacc = cv.get([128, 2, 2048], F32)
    Lacc = cv.get([128, 2, 2048], F32)
    sb_rot = Rot("sb", [cv.get([128, 384], F32) for _ in range(4)])
    P_rot = Rot("P", [cv.get([128, 384], BF16) for _ in range(4)])
    rl_a = Qa.rearrange("p h n -> p (h n)").bitcast(F32)
    oa_bf = Ka[:, 0, 0:2048]

    R.dma("sp", lambda e: e.dma_start(out=abias, in_=abias_d), writes=["abias"])
    psS_rot = Rot("ps", [(i, psb[i]) for i in range(0, 4)])
    psOL_rot = Rot("ps", [(4, 5), (6, 7)])

    def a_unit(g, h, qcol, segs, asl):
        S, Sk = next_ps(psS_rot)
        for si, (kcol, nk_s, vap, bt) in enumerate(segs):
            R.pe(lambda e, si=si, kcol=kcol, nk_s=nk_s: e.matmul(
                S[0:nk_s, si * 128:(si + 1) * 128], Ka[:, h, kcol:kcol + nk_s], Qa[:, h, qcol:qcol + 128],
                start=True, stop=True), reads=["Ka", "Qa"], writes=[Sk])
        yield
        sb, sbk = sb_rot.next()
        for si, (kcol, nk_s, vap, bt) in enumerate(segs):
            R.dve(lambda e, si=si, nk_s=nk_s, bt=bt: e.scalar_tensor_tensor(
                sb[0:nk_s, si * 128:(si + 1) * 128], S[0:nk_s, si * 128:(si + 1) * 128], SCALE,
                abias[0:nk_s, bt * 128:(bt + 1) * 128], ALU.mult, ALU.add),
                reads=[Sk, "abias"], writes=[(sbk, si)])
        yield
        Pt, Pk = P_rot.next()
        for si, (kcol, nk_s, vap, bt) in enumerate(segs):
            R.act(lambda e, si=si, nk_s=nk_s: e.activation(
                Pt[0:nk_s, si * 128:(si + 1) * 128], sb[0:nk_s, si * 128:(si + 1) * 128], AF.Exp),
                reads=[(sbk, si)], writes=[(Pk, si)])
        yield
        (io, il), _ = psOL_rot.next()
        O, Ok = psb[io], ("ps", io)
        L, Lk = psb[il], ("ps", il)
        ns = len(segs)
        for si, (kcol, nk_s, vap, bt) in enumerate(segs):
            R.pe(lambda e, si=si, nk_s=nk_s, vap=vap: e.matmul(
                O[:, 0:128], vap, Pt[0:nk_s, si * 128:(si + 1) * 128], start=(si == 0), stop=(si == ns - 1)),
                reads=["Va", (Pk, si)], writes=[Ok])
        for si, (kcol, nk_s, vap, bt) in enumerate(segs):
            R.pe(lambda e, si=si, nk_s=nk_s: e.matmul(
                L[:, 0:128], ones[0:nk_s, :], Pt[0:nk_s, si * 128:(si + 1) * 128], start=(si == 0), stop=(si == ns - 1)),
                reads=["ones", (Pk, si)], writes=[Lk])
        yield
        ukey = ("Uacc", h)
        lkey = ("Lacc", h)
        if g == 0:
            R.dve(lambda e: e.tensor_copy(Uacc[:, h, asl], O[:, 0:128]), reads=[Ok], writes=[ukey])
            R.dve(lambda e: e.tensor_copy(Lacc[:, h, asl], L[:, 0:128]), reads=[Lk], writes=[lkey])
        else:
            R.dve(lambda e: e.tensor_tensor(Uacc[:, h, asl], Uacc[:, h, asl], O[:, 0:128], ALU.add), reads=[Ok, ukey], writes=[ukey])
            R.dve(lambda e: e.tensor_tensor(Lacc[:, h, asl], Lacc[:, h, asl], L[:, 0:128], ALU.add), reads=[Lk, lkey], writes=[lkey])

    for g, d in enumerate(DILS):
        Lo, Lh = NOWN // d, 1024 // d
        Lr = Lo + Lh
        R.dma("sp", lambda e, g=g: e.dma_start(out=Qa, in_=QA_d[g].rearrange("h p n -> p h n")), reads=[("QA_d", g)], writes=["Qa"])
        R.dma("sp", lambda e, g=g: e.dma_start(out=Ka, in_=KA_d[g].rearrange("h p n -> p h n")), reads=[("KA_d", g)], writes=["Ka"])
        if g < 2:
            R.dma("sp", lambda e, g=g: e.dma_start(out=Va[:, 0:24, :], in_=VA_d[g].rearrange("(t p) c -> p t c", p=128)),
                  reads=[("VA_d", g)], writes=["Va"])
        else:
            vv = VA_d[g].rearrange("(c r) x -> c r x", r=192)
            R.dma("sp", lambda e, vv=vv: e.dma_start(out=Va[:, 0:16, :], in_=vv[:, 0:128, :].rearrange("c p x -> p c x")),
                  reads=[("VA_d", g)], writes=["Va"])
            R.dma("sp", lambda e, vv=vv: e.dma_start(out=Va[0:64, 16:32, :], in_=vv[:, 128:192, :].rearrange("c p x -> p c x")),
                  reads=[("VA_d", g)], writes=["Va"])
        aunits = []
        for h in range(2):
            if g < 2:
                nq, nk = Lo // 128, Lr // 128
                bbase = (g * 2 + h) * 5
                for cl in range(d):
                    for qi in range(nq):
                        segs = []
                        for r, kt in enumerate(((qi - 1) % nk, qi, (qi + 1) % nk)):
                            bt = bbase + r
                            if r == 0 and qi == 0:
                                bt = bbase + 3
                            if r == 2 and qi == nq - 1:
                                bt = bbase + 4
                            segs.append((cl * Lr + kt * 128, 128, Va[:, cl * nk + kt, h * 128:(h + 1) * 128], bt))
                        a0 = cl + d * qi * 128
                        asl = slice(a0, a0 + 128) if d == 1 else slice(a0, min(a0 + d * 128, 2048), d)
                        aunits.append(a_unit(g, h, cl * Lo + qi * 128, segs, asl))
            else:
                bbase = 20 + h * 2
                for cl in range(16):
                    segs = [(cl * 192, 128, Va[:, cl, h * 128:(h + 1) * 128], bbase),
                            (cl * 192 + 128, 64, Va[0:64, 16 + cl, h * 128:(h + 1) * 128], bbase + 1)]
                    aunits.append(a_unit(g, h, cl * 128, segs, slice(cl, 2048, 16)))
        run_pipeline(aunits)
    for h in range(2):
        R.dve(lambda e, h=h: e.reciprocal(rl_a, Lacc[:, h, :]), reads=[("Lacc", h)], writes=["Qa"])
        R.dve(lambda e, h=h: e.tensor_tensor(oa_bf, Uacc[:, h, :], rl_a, ALU.mult), reads=[("Uacc", h), "Qa"], writes=["Ka"])
        R.dma("sp", lambda e, h=h: e.dma_start(out=OT_d[h], in_=oa_bf), reads=["Ka"], writes=["OT_d"])

    KBs = cv.get([128, 2, 4096], BF16)
    VBs = cv.get([128, 32, 256], BF16)
    QBs = cv.get([128, 6, 2048], BF16)
    KCs = cv.get([128, 4, 256], BF16)
    VCs = cv.get([128, 2, 512], BF16)
    QCs = cv.get([128, 4, 2048], BF16)
    P2_rot = Rot("P2", [cv.get([128, 512], BF16) for _ in range(4)])
    rl_rot = Rot("rl", [cv.get([128, 512], F32) for _ in range(2)])
    ob_rot = Rot("ob", [cv.get([128, 512], BF16) for _ in range(2)])

    R.dma("sp", lambda e: e.dma_start(out=KBs, in_=KB_d.rearrange("h p n -> p h n")), reads=["KB_d"], writes=["KBs"])
    R.dma("sp", lambda e: e.dma_start(out=VBs, in_=VB_d.rearrange("(t p) c -> p t c", p=128)), reads=["VB_d"], writes=["VBs"])
    R.dma("sp", lambda e: e.dma_start(out=QBs, in_=QB_d.rearrange("h p n -> p h n")), reads=["QB_d"], writes=["QBs"])
    R.dma("sp", lambda e: e.dma_start(out=KCs, in_=KC_d.rearrange("h p n -> p h n")), reads=["KC_d"], writes=["KCs"])
    R.dma("sp", lambda e: e.dma_start(out=VCs, in_=VC_d.rearrange("(t p) c -> p t c", p=128)), reads=["VC_d"], writes=["VCs"])
    R.dma("sp", lambda e: e.dma_start(out=QCs, in_=QC_d.rearrange("h p n -> p h n")), reads=["QC_d"], writes=["QCs"])

    psS2_rot = Rot("ps", [(i, psb[i]) for i in range(0, 4)])
    psOL2_rot = Rot("ps", [(4, 5), (6, 7)])

    def dense_attn(Ks, Kkey, kh, Vs, Vkey, vcol, nkt, Qs, Qkey, qh, qb, ot_idx):
        q_ap = Qs[:, qh, qb * 512:(qb + 1) * 512]
        (io, il), _ = psOL2_rot.next()
        O, Ok = psb[io], ("ps", io)
        L, Lk = psb[il], ("ps", il)
        pend = []

        def issue_s(kt):
            S, Sk = next_ps(psS2_rot)
            R.pe(lambda e, S=S, kt=kt: e.matmul(S, Ks[:, kh, kt * 128:(kt + 1) * 128], q_ap, start=True, stop=True),
                 reads=[Kkey, Qkey], writes=[Sk])
            Pt, Pk = P2_rot.next()
            R.act(lambda e, S=S, Pt=Pt: e.activation(Pt, S, AF.Exp, scale=SCALE), reads=[Sk], writes=[Pk])
            pend.append((Pt, Pk))

        issue_s(0)
        if nkt > 1:
            issue_s(1)
        for kt in range(nkt):
            if kt + 2 < nkt:
                issue_s(kt + 2)
            Pt, Pk = pend.pop(0)
            pump_every(8, pace=[Pk])
            R.pe(lambda e, Pt=Pt, kt=kt: e.matmul(O, Vs[:, kt, vcol:vcol + 128], Pt, start=(kt == 0), stop=(kt == nkt - 1)),
                 reads=[Vkey, Pk], writes=[Ok])
            R.pe(lambda e, Pt=Pt, kt=kt: e.matmul(L, ones, Pt, start=(kt == 0), stop=(kt == nkt - 1)),
                 reads=["ones", Pk], writes=[Lk])
        rl, rlk = rl_rot.next()
        ob, obk = ob_rot.next()
        R.dve(lambda e: e.reciprocal(rl, L), reads=[Lk], writes=[rlk])
        R.dve(lambda e: e.tensor_tensor(ob, O, rl, ALU.mult), reads=[Ok, rlk], writes=[obk])
        R.dma("sp", lambda e: e.dma_start(out=OT_d[ot_idx, :, qb * 512:(qb + 1) * 512], in_=ob), reads=[obk], writes=["OT_d"])

    for j in range(2):
        for qb in range(4):
            for gq in range(3):
                h = 3 * j + gq
                dense_attn(KBs, "KBs", j, VBs, "VBs", j * 128, 32, QBs, "QBs", h, qb, 2 + h)
    for h in range(4):
        for qb in range(4):
            dense_attn(KCs, "KCs", h, VCs, "VCs", h * 128, 2, QCs, "QCs", h, qb, 8 + h)

    R.barrier()

    cv = Carver(PERS)
    x1 = cv.get([128, 4, D], F32)
    hTb = cv.get([128, 16, 512], BF16)
    uT = cv.get([128, 44, 512], BF16)
    OTb = cv.get([128, 12, 512], BF16)
    mT = uT[:, 12:28, :]
    wblk_rot = Rot("wblk", [cv.get([128, 16, 512], BF16) for _ in range(4)])
    xs_rot = Rot("xs", [cv.get([128, D], BF16) for _ in range(1)])
    junk = cv.get([128, D], BF16)
    sg_rot = Rot("sgm", [cv.get([128, 512], F32) for _ in range(2)])
    maccs = [cv.get([128, 512], F32) for _ in range(4)]
    tmp_rot = Rot("tmp", [cv.get([128, 512], F32) for _ in range(2)])
    ostg_rot = Rot("ostg", [cv.get([128, 512], F32) for _ in range(2)])

    psG_rot = Rot("ps", [(i, psb[i]) for i in range(0, 8)])
    pst4_rot = Rot("ps", [(6, psb[6].bitcast(BF16)), (7, psb[7].bitcast(BF16))])

    def load_w(src_blk, k, key):
        ensure_conv(key)
        wb, wk = wblk_rot.next()
        R.dma("pool", lambda e: e.dma_start(out=wb[:, 0:k, :], in_=src_blk[:, 0:k * 512].rearrange("p (k c) -> p k c", k=k)),
              reads=[(key, k0) for k0 in range(0, k, 4)], writes=[wk])
        return wb, wk

    def load_block_acts(t0):
        R.dma("sp", lambda e: e.dma_start(out=hTb, in_=hT_own_d[:, :, t0:t0 + 512].rearrange("k p t -> p k t")),
              reads=["hT_own_d"], writes=["hTb"])
        R.dma("sp", lambda e: e.dma_start(out=OTb, in_=OT_d[:, :, t0:t0 + 512].rearrange("k p t -> p k t")),
              reads=["OT_d"], writes=[("OTb", i) for i in range(12)])

    load_block_acts(0)
    for tb in range(4):
        t0 = tb * 512
        R.dma("sp", lambda e, t0=t0: e.dma_start(out=x1, in_=x_own[t0:t0 + 512, :].rearrange("(t p) c -> p t c", p=128)),
              writes=[("x1", t) for t in range(4)])
        for cq in range(4):
            for bi in range(3):
                f0, nch = br_rows[bi]
                wg, wgk = load_w(WG_d[bi * 4 + cq], 16, ("WG", bi * 4 + cq))
                wbr, wbrk = load_w(WBR_d[bi * 4 + cq], nch, ("WBR", bi * 4 + cq))
                for cc in range(4):
                    c = cq * 4 + cc
                    macc = maccs[cc]
                    mk = ("macc", cc)
                    G, Gk = next_ps(psG_rot)
                    for kc in range(16):
                        R.pe(lambda e, G=G, wg=wg, kc=kc, cc=cc: e.matmul(
                            G, wg[:, kc, cc * 128:(cc + 1) * 128], hTb[:, kc, :], start=(kc == 0), stop=(kc == 15)),
                            reads=[wgk, "hTb"], writes=[Gk])
                    Bp, Bk = next_ps(psG_rot)
                    for i in range(nch):
                        R.pe(lambda e, Bp=Bp, wbr=wbr, i=i, f0=f0, nch=nch, cc=cc: e.matmul(
                            Bp, wbr[:, i, cc * 128:(cc + 1) * 128], OTb[:, f0 + i, :], start=(i == 0), stop=(i == nch - 1)),
                            reads=[wbrk] + [("OTb", f0 + i)], writes=[Bk])
                    pump_conv(1)
                    sg, sgk = sg_rot.next()
                    R.act(lambda e, sg=sg, G=G: e.activation(sg, G, AF.Sigmoid), reads=[Gk], writes=[sgk])
                    if bi == 0:
                        R.dve(lambda e, sg=sg, Bp=Bp, macc=macc: e.tensor_tensor(macc, sg, Bp, ALU.mult), reads=[sgk, Bk], writes=[mk])
                    elif bi == 1:
                        tm, tmk = tmp_rot.next()
                        R.dve(lambda e, sg=sg, Bp=Bp, tm=tm: e.tensor_tensor(tm, sg, Bp, ALU.mult), reads=[sgk, Bk], writes=[tmk])
                        R.dve(lambda e, tm=tm, macc=macc: e.tensor_tensor(macc, macc, tm, ALU.add), reads=[tmk, mk], writes=[mk])
                    else:
                        tm, tmk = tmp_rot.next()
                        R.dve(lambda e, sg=sg, Bp=Bp, tm=tm: e.tensor_tensor(tm, sg, Bp, ALU.mult), reads=[sgk, Bk], writes=[tmk])
                        R.dve(lambda e, tm=tm, c=c, macc=macc: e.tensor_tensor(mT[:, c, :], macc, tm, ALU.add),
                              reads=[tmk, mk], writes=[("uT", 12 + c)])
        for cb in range(4):
            wb, wk = load_w(WO_d[cb], 16, ("WO", cb))
            for tt in range(4):
                ps, psk = next_ps(psG_rot)
                for c in range(16):
                    R.pe(lambda e, ps=ps, wb=wb, c=c, tt=tt: e.matmul(
                        ps, mT[:, c, tt * 128:(tt + 1) * 128], wb[:, c, :], start=(c == 0), stop=(c == 15)),
                        reads=[wk] + [("uT", 12 + c)], writes=[psk])
                R.dve(lambda e, ps=ps, tt=tt, cb=cb: e.tensor_tensor(
                    x1[:, tt, cb * 512:(cb + 1) * 512], x1[:, tt, cb * 512:(cb + 1) * 512], ps, ALU.add),
                    reads=[psk, ("x1", tt)], writes=[("x1", tt)])
        run_pipeline([norm_unit(None, x1[:, tt, :], ("x1", tt), 16, hTb, "hTb", tt * 128, pst4_rot, xs_rot, junk)
                      for tt in range(4)])
        for jg in range(11):
            wa, wak = load_w(WFI_d[2 * jg], 16, ("WFI", 2 * jg))
            wbb, wbk = load_w(WFI_d[2 * jg + 1], 16, ("WFI", 2 * jg + 1))
            for jj in range(4):
                j = jg * 4 + jj
                Pa, Pak = next_ps(psG_rot)
                for kc in range(16):
                    R.pe(lambda e, Pa=Pa, wa=wa, kc=kc, jj=jj: e.matmul(
                        Pa, wa[:, kc, jj * 128:(jj + 1) * 128], hTb[:, kc, :], start=(kc == 0), stop=(kc == 15)),
                        reads=[wak, "hTb"], writes=[Pak])
                Pb, Pbk = next_ps(psG_rot)
                for kc in range(16):
                    R.pe(lambda e, Pb=Pb, wbb=wbb, kc=kc, jj=jj: e.matmul(
                        Pb, wbb[:, kc, jj * 128:(jj + 1) * 128], hTb[:, kc, :], start=(kc == 0), stop=(kc == 15)),
                        reads=[wbk, "hTb"], writes=[Pbk])
                sg, sgk = sg_rot.next()
                R.act(lambda e, sg=sg, Pa=Pa: e.activation(sg, Pa, AF.Silu), reads=[Pak], writes=[sgk])
                R.dve(lambda e, sg=sg, Pb=Pb, j=j: e.tensor_tensor(uT[:, j, :], sg, Pb, ALU.mult),
                      reads=[sgk, Pbk], writes=[("uT", j)])
        if tb + 1 < 4:
            load_block_acts(t0 + 512)
        for cb in range(4):
            accs = [next_ps(psG_rot) for _ in range(4)]
            for kp in range(4):
                wb, wk = load_w(WFO_d[cb * 4 + kp], 11, ("WFO", cb * 4 + kp))
                for tt in range(4):
                    ps, psk = accs[tt]
                    for i in range(11):
                        j = kp * 11 + i
                        R.pe(lambda e, ps=ps, wb=wb, i=i, j=j, tt=tt, kp=kp: e.matmul(
                            ps, uT[:, j, tt * 128:(tt + 1) * 128], wb[:, i, :], start=(kp == 0 and i == 0), stop=(kp == 3 and i == 10)),
                            reads=[wk, ("uT", j)], writes=[psk])
            for tt in range(4):
                ps, psk = accs[tt]
                og, ogk = ostg_rot.next()
                R.dve(lambda e, ps=ps, og=og, tt=tt, cb=cb: e.tensor_tensor(og, x1[:, tt, cb * 512:(cb + 1) * 512], ps, ALU.add),
                      reads=[psk, ("x1", tt)], writes=[ogk])
                R.dma("sp", lambda e, og=og, tt=tt, cb=cb, t0=t0: e.dma_start(
                    out=out_d[t0 + tt * 128:t0 + (tt + 1) * 128, cb * 512:(cb + 1) * 512], in_=og), reads=[ogk], writes=["out"])

    with contextlib.ExitStack() as es:
        sems = {n: es.enter_context(nc.semaphore("s_" + n)) for n in ("pe", "act", "dve", "pool")}
        sems["dma"] = {"sp": [es.enter_context(nc.semaphore(f"dsp{i}")) for i in range(24)],
                       "pool": [es.enter_context(nc.semaphore(f"dpl{i}")) for i in range(12)],
                       "pool:cv": [es.enter_context(nc.semaphore(f"dcv{i}")) for i in range(6)]}
        block = es.enter_context(nc.Block())
        cc = R.prepare(sems)

        @block.tensor
        def _(e):
            R.run_engine("pe", e, sems)

        @block.scalar
        def _(e):
            R.run_engine("act", e, sems)

        @block.vector
        def _(e):
            R.run_engine("dve", e, sems)

        @block.gpsimd
        def _(e):
            R.run_engine("pool", e, sems)
            R.final_waits("pool", e, sems)

        @block.sync
        def _(e):
            R.run_engine("sp", e, sems)
            R.final_waits("sp", e, sems)
    build_program.R = R
    return nc, len(R.ops), cc


def _t5_bucket(rel):
    nb = 16
    ret = np.where(rel > 0, nb, 0)
    n = np.abs(rel)
    max_exact = 8
    large = max_exact + (np.log(np.maximum(n, 1).astype(np.float32) / max_exact)
                         / math.log(1024 / max_exact) * (nb - max_exact)).astype(np.int32)
    large = np.minimum(large, nb - 1)
    return ret + np.where(n < max_exact, n, large)


def _rope_table(pos):
    r = (pos // 64).astype(np.float32)
    c = (pos % 64).astype(np.float32)
    inv = (10000.0 ** (-np.arange(32, dtype=np.float32) / 32)).astype(np.float32)
    ang = np.concatenate([r[:, None] * inv, c[:, None] * inv], axis=-1).astype(np.float32)
    return np.concatenate([np.cos(ang), np.sin(ang)], axis=-1).astype(np.float32)


def _bias_tiles(rel_bias, parity):
    tiles = np.full((24, 128, 128), NEGB, dtype=np.float32)
    kk = np.arange(128)[:, None]
    qq = np.arange(128)[None, :]

    def tile(rel, d, head, kvalid=None):
        valid = np.abs(rel) <= 64
        if kvalid is not None:
            valid = valid & kvalid
        b = _t5_bucket(rel * d)
        return np.where(valid, rel_bias[b, head], np.float32(NEGB)).astype(np.float32)

    for g, d in enumerate((1, 4)):
        for h in range(2):
            head = 2 * g + h
            base = (g * 2 + h) * 5
            tiles[base + 0] = tile(kk - 128 - qq, d, head)
            tiles[base + 1] = tile(kk - qq, d, head)
            tiles[base + 2] = tile(kk + 128 - qq, d, head)
            if parity == 1:
                tiles[base + 3] = tiles[base + 0]
            if parity == 0:
                tiles[base + 4] = tiles[base + 2]
    for h in range(2):
        head = 4 + h
        base = 20 + h * 2
        tiles[base + 0] = tile(kk - qq, 16, head)
        k64 = (kk < 64)
        if parity == 0:
            rel = (128 + kk) - qq
        else:
            rel = (kk - 64) - qq
        t = tile(rel, 16, head, kvalid=k64)
        tiles[base + 1] = t
    return np.ascontiguousarray(tiles.transpose(1, 0, 2).reshape(128, 24 * 128))


_PROG = None


def kernel(x, mem, rel_bias, g_mix, w_in, g_qa, g_ka, g_qb, g_kb, g_mem, w_mem_kv, g_qc, g_kc,
           w_br_a, w_br_b, w_br_c, w_o, g_ffn, w_ffn_in, w_ffn_out):
    global _PROG
    if _PROG is None:
        _PROG = build_program()
    nc = _PROG[0]
    in_maps = make_in_maps(x, mem, rel_bias, g_mix, w_in, g_qa, g_ka, g_qb, g_kb, g_mem, w_mem_kv, g_qc, g_kc,
                           w_br_a, w_br_b, w_br_c, w_o, g_ffn, w_ffn_in, w_ffn_out)
    res = run_bass_kernel_spmd(nc, in_maps, core_ids=list(range(8)))
    out = np.empty((4, SEQ, D), dtype=np.float32)
    for core in range(8):
        b, par = core // 2, core % 2
        out[b, par * 2048:(par + 1) * 2048] = res.results[core]["out"]
    return out


def make_in_maps(x, mem, rel_bias, g_mix, w_in, g_qa, g_ka, g_qb, g_kb, g_mem, w_mem_kv, g_qc, g_kc,
                 w_br_a, w_br_b, w_br_c, w_o, g_ffn, w_ffn_in, w_ffn_out):
    f = lambda a: np.ascontiguousarray(np.asarray(a, dtype=np.float32))
    x = f(x)
    mem = f(mem)
    rel_bias = f(rel_bias)
    col = lambda g: f(g)[0].reshape(16, 128).T
    gcol = np.ascontiguousarray(np.concatenate([col(g_mix), col(g_ffn), col(g_mem)], axis=1))
    ghead = np.ascontiguousarray(np.broadcast_to(
        np.concatenate([f(g)[0] for g in (g_qa, g_ka, g_qb, g_kb, g_qc, g_kc)])[None, :], (128, 768)))
    ident = np.eye(128, dtype=np.float32)
    shared = {
        "w_in": f(w_in)[0], "w_mem": f(w_mem_kv)[0], "w_br_a": f(w_br_a)[0], "w_br_b": f(w_br_b)[0],
        "w_br_c": f(w_br_c)[0], "w_o": f(w_o)[0], "w_fi": f(w_ffn_in)[0], "w_fo": f(w_ffn_out)[0],
        "gcol": gcol, "ghead": ghead, "ident": ident,
    }
    in_maps = []
    for core in range(8):
        b, par = core // 2, core % 2
        if par == 0:
            own = x[b, 0:2048]
            oth = x[b, 2048:4096]
            pos_own = np.arange(0, 2048)
            pos_oth = np.arange(2048, 4096)
        else:
            own = x[b, 2048:4096]
            oth = np.concatenate([x[b, 1024:2048], x[b, 0:1024]], axis=0)
            pos_own = np.arange(2048, 4096)
            pos_oth = np.concatenate([np.arange(1024, 2048), np.arange(0, 1024)])
        m = dict(shared)
        m["x_own"] = np.ascontiguousarray(own)
        m["x_oth"] = np.ascontiguousarray(oth)
        m["memx"] = np.ascontiguousarray(mem[b])
        m["rope_own"] = np.ascontiguousarray(_rope_table(pos_own).reshape(16, 128, 128).transpose(1, 0, 2))
        m["rope_oth"] = np.ascontiguousarray(_rope_table(pos_oth).reshape(16, 128, 128).transpose(1, 0, 2))
        m["abias"] = _bias_tiles(rel_bias, par)
        in_maps.append(m)
    return in_maps
```

```python
import contextlib
import math
import numpy as np
import concourse.bass as bass
import concourse.mybir as mybir
from concourse.bass_utils import run_bass_kernel_spmd

F32 = mybir.dt.float32
BF16 = mybir.dt.bfloat16
U8 = mybir.dt.uint8
AF = mybir.ActivationFunctionType
ALU = mybir.AluOpType
AX = mybir.AxisListType

D = 2048
SEQ = 4096
NOWN = 2048
DFF = 5632
EPS = 1e-6
SCALE = 1.0 / math.sqrt(128.0)
NEGB = -30000.0
DILS = (1, 4, 16)
ENG_NAMES = ("pe", "act", "dve", "pool", "sp")


class Op:
    __slots__ = ("eng", "fn", "reads", "writes", "dma", "deps", "need_inc", "cnt", "idx", "bar", "spool")

    def __init__(self, eng, fn, reads, writes, dma):
        self.eng = eng
        self.fn = fn
        self.reads = reads
        self.writes = writes
        self.dma = dma
        self.deps = ()
        self.need_inc = False
        self.cnt = 0
        self.bar = False
        self.spool = eng


class Rec:
    def __init__(self):
        self.ops = []

    disabled = False

    def op(self, eng, fn, reads=(), writes=(), dma=False):
        if self.disabled:
            return None
        o = Op(eng, fn, tuple(reads), tuple(writes), dma)
        o.idx = len(self.ops)
        self.ops.append(o)
        return o

    def pe(self, fn, reads=(), writes=()):
        return self.op("pe", fn, reads, writes)

    def act(self, fn, reads=(), writes=()):
        return self.op("act", fn, reads, writes)

    def dve(self, fn, reads=(), writes=()):
        return self.op("dve", fn, reads, writes)

    def pool(self, fn, reads=(), writes=()):
        return self.op("pool", fn, reads, writes)

    def dma(self, eng, fn, reads=(), writes=(), spool=None):
        o = self.op(eng, fn, reads, writes, dma=True)
        if o is not None and spool is not None:
            o.spool = spool
        return o

    def barrier(self):
        if self.disabled:
            return
        for e in ENG_NAMES:
            o = self.op(e, None)
            o.bar = True

    def analyze(self):
        last_w = {}
        readers = {}
        last_on = {}
        dmas_since = []
        i = 0
        n = len(self.ops)
        while i < n:
            o = self.ops[i]
            if o.bar:
                grp = []
                while i < n and self.ops[i].bar:
                    grp.append(self.ops[i])
                    i += 1
                forced = [v for v in last_on.values()] + list(dmas_since)
                for b in grp:
                    b.deps = tuple(sorted(forced))
                for j in forced:
                    self.ops[j].need_inc = True
                last_w = {}
                readers = {}
                dmas_since = []
                continue
            deps = set()
            for k in o.reads:
                w = last_w.get(k)
                if w is not None:
                    deps.add(w)
            for k in o.writes:
                w = last_w.get(k)
                if w is not None:
                    deps.add(w)
                for r in readers.get(k, ()):
                    deps.add(r)
            deps.discard(o.idx)
            keep = []
            for j in deps:
                p = self.ops[j]
                if (not p.dma) and (not o.dma) and p.eng == o.eng:
                    if o.eng == "pe":
                        continue
                keep.append(j)
            best = {}
            kept = []
            for j in keep:
                p = self.ops[j]
                if p.dma:
                    kept.append(j)
                elif j > best.get(p.eng, -1):
                    best[p.eng] = j
            o.deps = tuple(sorted(kept + list(best.values())))
            for j in o.deps:
                self.ops[j].need_inc = True
            for k in o.reads:
                readers.setdefault(k, []).append(o.idx)
            for k in o.writes:
                last_w[k] = o.idx
                readers[k] = []
            if o.dma:
                dmas_since.append(o.idx)
            else:
                last_on[o.eng] = o.idx
            i += 1

    def prepare(self, sems):
        self.analyze()
        ccount = {e: 0 for e in ENG_NAMES}
        dma_sems = sems["dma"]
        self.dma_cnt = {e: [0] * len(dma_sems[e]) for e in dma_sems}
        rr = {e: 0 for e in dma_sems}
        for o in self.ops:
            if o.bar:
                continue
            if o.dma:
                nse = len(dma_sems[o.spool])
                s = rr[o.spool] % nse
                rr[o.spool] += 1
                self.dma_cnt[o.spool][s] += 16
                o.cnt = (o.spool, s, self.dma_cnt[o.spool][s])
            elif o.need_inc:
                ccount[o.eng] += 1
                o.cnt = ccount[o.eng]
        self.streams = {e: [] for e in ENG_NAMES}
        for o in self.ops:
            self.streams[o.eng].append(o)
        return ccount

    def run_engine(self, ename, eng, sems):
        waited = {}
        dma_sems = sems["dma"]
        for o in self.streams[ename]:
            need = {}
            for j in o.deps:
                p = self.ops[j]
                if p.dma:
                    pe_, s, v = p.cnt
                    key = ("d", pe_, s)
                else:
                    key = p.eng
                    v = p.cnt
                if v > need.get(key, 0):
                    need[key] = v
            if o.dma:
                sp_, s, v = o.cnt
                key = ("d", sp_, s)
                if v > 16 and need.get(key, 0) < v - 16:
                    need[key] = v - 16
            for key, v in need.items():
                if waited.get(key, 0) >= v:
                    continue
                if isinstance(key, tuple):
                    eng.wait_ge(dma_sems[key[1]][key[2]], v)
                else:
                    eng.wait_ge(sems[key], v)
                waited[key] = v
            if o.bar:
                continue
            ins = o.fn(eng)
            if o.dma:
                sp_, s, v = o.cnt
                ins.then_inc(dma_sems[sp_][s], 16)
            elif o.need_inc:
                ins.then_inc(sems[ename], 1)

    def final_waits(self, ename, eng, sems):
        for pool_name, cnts in self.dma_cnt.items():
            if pool_name.split(":")[0] != ename:
                continue
            for s, v in enumerate(cnts):
                if v > 0:
                    eng.wait_ge(sems["dma"][pool_name][s], v)


class Rot:
    def __init__(self, name, aps, track=False):
        self.name = name
        self.aps = aps
        self.i = 0
        self.track = track
        self.held = [False] * len(aps)

    def next(self):
        n = len(self.aps)
        if not self.track:
            i = self.i % n
            self.i += 1
            return self.aps[i], (self.name, i)
        for d in range(n):
            i = (self.i + d) % n
            if not self.held[i]:
                self.held[i] = True
                self.i = i + 1
                return self.aps[i], (self.name, i)
        raise RuntimeError(f"rotation {self.name} exhausted ({n} buffers)")

    def release(self, key):
        self.held[key[1]] = False


def build_program(stop_after=None, rope_eng="pool", dbg=None):
    nc = bass.Bass("TRN2", target_bir_lowering=False)
    R = Rec()
    dbg = dbg or {}

    def din(name, shape, dt=F32):
        return nc.dram_tensor(name, list(shape), dt, kind="ExternalInput").ap()

    x_own = din("x_own", [NOWN, D])
    x_oth = din("x_oth", [NOWN, D])
    memx = din("memx", [256, D])
    w_in = din("w_in", [D, 10240])
    w_mem = din("w_mem", [D, 1024])
    w_br = [din("w_br_a", [256, D]), din("w_br_b", [768, D]), din("w_br_c", [512, D])]
    w_o = din("w_o", [D, D])
    w_fi = din("w_fi", [D, 2 * DFF])
    w_fo = din("w_fo", [DFF, D])
    gcol_d = din("gcol", [128, 48])
    ghead_d = din("ghead", [128, 768])
    rope_own_d = din("rope_own", [128, 16, 128])
    rope_oth_d = din("rope_oth", [128, 16, 128])
    abias_d = din("abias", [128, 24 * 128])
    ident_d = din("ident", [128, 128])
    out_d = nc.dram_tensor("out", [NOWN, D], F32, kind="ExternalOutput").ap()

    def scr(name, shape):
        return nc.dram_tensor(name, list(shape), BF16).ap()

    hT_own_d = scr("hT_own", [16, 128, NOWN])
    QA_d = [scr(f"QA{g}", [2, 128, 2048]) for g in range(3)]
    KA_d = [scr(f"KA{g}", [2, 128, 3072]) for g in range(3)]
    VA_d = [scr(f"VA{g}", [3072, 256]) for g in range(3)]
    QB_d = scr("QB", [6, 128, 2048])
    KB_d = scr("KB", [2, 128, 4096])
    VB_d = scr("VB", [4096, 256])
    QC_d = scr("QC", [4, 128, 2048])
    KC_d = scr("KC", [4, 128, 256])
    VC_d = scr("VC", [256, 512])
    OT_d = scr("OT", [12, 128, NOWN])
    WG_d = scr("WG", [12, 128, 16 * 512])
    WBR_d = scr("WBR", [12, 128, 6 * 512])
    WO_d = scr("WO", [4, 128, 16 * 512])
    WFI_d = scr("WFI", [22, 128, 16 * 512])
    WFO_d = scr("WFO", [16, 128, 11 * 512])

    ARENA = 206 * 1024
    arena = nc.alloc_sbuf_tensor("arena", [128, ARENA], U8).ap()

    class Carver:
        def __init__(self, base):
            self.off = base

        def get(self, shape, dt):
            esz = 4 if dt == F32 else 2
            nb = int(np.prod(shape[1:])) * esz
            nb_al = (nb + 63) // 64 * 64
            assert self.off + nb_al <= ARENA, (self.off, nb_al)
            ap = arena[:, self.off:self.off + nb].bitcast(dt)
            self.off += nb_al
            if len(shape) == 3:
                ap = ap.rearrange("p (a b) -> p a b", a=shape[1])
            return ap

    cv = Carver(0)
    ident = cv.get([128, 128], BF16)
    ones = cv.get([128, 128], BF16)
    gcol = cv.get([128, 48], F32)
    ghead = cv.get([128, 768], F32)
    stat_rot = Rot("st", [cv.get([128, 16], F32) for _ in range(16)], track=True)
    PERS = cv.off

    psb = [nc.alloc_psum_tensor(f"ps{i}", [128, 512], F32).ap() for i in range(8)]

    br_rows = [(0, 2), (2, 6), (8, 4)]
    conv_list = []
    conv_last = {}

    def add_conv(dst, k, src, key):
        for k0 in range(0, k, 4):
            kn = min(4, k - k0)
            conv_last[key] = len(conv_list)
            conv_list.append(lambda k0=k0, kn=kn, pace=(): R.dma("pool", lambda e: e.dma_start(
                out=dst[:, k0 * 512:(k0 + kn) * 512].rearrange("p (k c) -> p k c", k=kn),
                in_=src[k0 * 128:(k0 + kn) * 128, :].rearrange("(k p) c -> p k c", p=128)),
                reads=list(pace), writes=[(key, k0)], spool="pool:cv"))

    for cq in range(4):
        for bi in range(3):
            add_conv(WG_d[bi * 4 + cq], 16, w_in[:, 4096 + bi * 2048 + cq * 512:4096 + bi * 2048 + (cq + 1) * 512], ("WG", bi * 4 + cq))
            add_conv(WBR_d[bi * 4 + cq], br_rows[bi][1], w_br[bi][:, cq * 512:(cq + 1) * 512], ("WBR", bi * 4 + cq))
    for cb in range(4):
        add_conv(WO_d[cb], 16, w_o[:, cb * 512:(cb + 1) * 512], ("WO", cb))
    for jg in range(11):
        add_conv(WFI_d[2 * jg], 16, w_fi[:, jg * 512:(jg + 1) * 512], ("WFI", 2 * jg))
        add_conv(WFI_d[2 * jg + 1], 16, w_fi[:, DFF + jg * 512:DFF + (jg + 1) * 512], ("WFI", 2 * jg + 1))
    for cb in range(4):
        for kp in range(4):
            add_conv(WFO_d[cb * 4 + kp], 11, w_fo[kp * 1408:(kp + 1) * 1408, cb * 512:(cb + 1) * 512], ("WFO", cb * 4 + kp))
    conv_pos = [0]

    def pump_conv(n, pace=()):
        while n > 0 and conv_pos[0] < len(conv_list):
            conv_list[conv_pos[0]](pace=pace)
            conv_pos[0] += 1
            n -= 1

    def ensure_conv(key):
        while conv_pos[0] <= conv_last[key]:
            pump_conv(1)

    pump_tick = [0]

    def pump_every(n):
        pump_tick[0] += 1
        if pump_tick[0] % n == 0:
            pump_conv(1)

    R.dma("pool", lambda e: e.dma_start(out=ident, in_=ident_d), writes=["ident"])
    R.dve(lambda e: e.memset(ones, 1.0), writes=["ones"])
    R.dma("sp", lambda e: e.dma_start(out=gcol, in_=gcol_d), writes=["gcol"])
    R.dma("sp", lambda e: e.dma_start(out=ghead, in_=ghead_d), writes=["ghead"])

    def run_pipeline(units):
        active = []
        it = iter(units)
        pending = True
        while pending or active:
            nxt = []
            for gen in active:
                try:
                    next(gen)
                    nxt.append(gen)
                except StopIteration:
                    pass
            active = nxt
            if pending:
                try:
                    u = next(it)
                    try:
                        next(u)
                        active.append(u)
                    except StopIteration:
                        pass
                except StopIteration:
                    pending = False

    cv = Carver(PERS)
    rope_own = cv.get([128, 16, 128], F32)
    rope_oth = cv.get([128, 16, 128], F32)
    hT_bufs = [cv.get([128, 16, 1024], BF16) for _ in range(2)]
    xin_rot = Rot("xin", [cv.get([128, D], F32) for _ in range(2)], track=True)
    xs_rot = Rot("xs", [cv.get([128, D], BF16) for _ in range(2)], track=True)
    junk = cv.get([128, D], BF16)
    wblk_rot = Rot("wblk", [cv.get([128, 16, 512], BF16) for _ in range(3)])
    qf_rot = Rot("qf", [cv.get([128, 512], F32) for _ in range(10)], track=True)
    rt_rot = Rot("rt", [cv.get([128, 1024], F32) for _ in range(2)])
    qb_rot = Rot("qb16", [cv.get([128, 512], BF16) for _ in range(8)], track=True)
    sqb_rot = Rot("sqb", [cv.get([128, 512], BF16) for _ in range(3)], track=True)
    stg_rot = Rot("stg", [cv.get([128, 512], BF16) for _ in range(2)])
    vstg_rot = Rot("vstg", [cv.get([128, 512], BF16) for _ in range(2)])
    G1_END = cv.off

    R.dma("sp", lambda e: e.dma_start(out=rope_own, in_=rope_own_d), writes=["rope_own"])
    R.dma("sp", lambda e: e.dma_start(out=rope_oth, in_=rope_oth_d), writes=["rope_oth"])

    psA_rot = Rot("psA", [(i, psb[i]) for i in range(0, 5)], track=True)
    pst_rot = Rot("pst", [(i, psb[i].bitcast(BF16)) for i in (5, 6, 7)])

    def next_ps(rot):
        (i, ap), _ = rot.next()
        return ap, ("ps", i)

    def next_ps_t(rot):
        (i, ap), rk = rot.next()
        return ap, ("ps", i), rk

    def rstd_stage_ms(st, stk, ntok, n, inv_n, c_ss, tmp):
        R.dve(lambda e: e.tensor_scalar(st[:ntok, tmp:tmp + n], st[:ntok, c_ss:c_ss + n], inv_n, EPS, ALU.mult, ALU.add),
              reads=[stk], writes=[stk])

    def rstd_stage_lnexp(st, stk, ntok, n, tmp, c_out):
        R.act(lambda e: e.activation(st[:ntok, tmp:tmp + n], st[:ntok, tmp:tmp + n], AF.Ln), reads=[stk], writes=[stk])
        R.act(lambda e: e.activation(st[:ntok, c_out:c_out + n], st[:ntok, tmp:tmp + n], AF.Exp, scale=-0.5),
              reads=[stk], writes=[stk])

    def norm_unit(load_fn, x_ap, x_key, goff, hT, hT_key, col0, pr, xsr, jk):
        loaded = load_fn is not None
        if loaded:
            x_ap, x_key = load_fn()
            yield
        st, stk = stat_rot.next()
        R.act(lambda e: e.activation(jk, x_ap, AF.Square, accum_out=st[:, 0:1]), reads=[x_key], writes=["junk", stk])
        yield
        rstd_stage_ms(st, stk, 128, 1, 1.0 / D, 0, 5)
        yield
        rstd_stage_lnexp(st, stk, 128, 1, 5, 1)
        xs, xsk = xsr.next()
        R.act(lambda e: e.activation(xs, x_ap, AF.Copy, scale=st[:, 1:2]), reads=[x_key, stk], writes=[xsk])
        stat_rot.release(stk)
        if loaded:
            xin_rot.release(x_key)
        yield
        for half in range(2):
            pt, ptk = next_ps(pr)
            for q in range(8):
                kc = half * 8 + q
                R.pe(lambda e, pt=pt, q=q, kc=kc: e.transpose(pt[:, q * 128:(q + 1) * 128], xs[:, kc * 128:(kc + 1) * 128], ident),
                     reads=[xsk, "ident"], writes=[ptk])
            R.dve(lambda e, pt=pt, half=half: e.tensor_tensor(
                hT[:, half * 8:half * 8 + 8, col0:col0 + 128], pt.rearrange("p (a b) -> p a b", a=8),
                gcol[:, goff + half * 8:goff + half * 8 + 8].unsqueeze(2).to_broadcast([128, 8, 128]), ALU.mult),
                reads=[ptk, "gcol"], writes=[hT_key])
        xsr.release(xsk)

    def load_unit(job):
        wb, wk = wblk_rot.next()
        off = 0
        for wap in job["wsegs"]:
            ncols = wap.shape[1]
            R.dma("pool", lambda e, wap=wap, off=off, ncols=ncols: e.dma_start(
                out=wb[:, :, off:off + ncols], in_=wap.rearrange("(k p) c -> p k c", p=128)), writes=[wk])
            off += ncols
        job["wb"], job["wk"], job["W"] = wb, wk, off
        return
        yield

    def proj_unit(job, hT, hT_key, csl, ntok, info):
        wb, wk, W = job["wb"], job["wk"], job["W"]
        jk = junk
        ps, psk, psrk = next_ps_t(psA_rot)
        for kc in range(16):
            R.pe(lambda e, kc=kc: e.matmul(ps[:ntok, :W], hT[:, kc, csl], wb[:, kc, :W], start=(kc == 0), stop=(kc == 15)),
                 reads=[hT_key, wk], writes=[psk])
        pump_every(4)
        yield
        qs = []
        for (coff, ncols, kind, gi, rope_fn, dest_fn) in job["segs"]:
            if kind == "v":
                vs, vsk = vstg_rot.next()
                R.act(lambda e, vs=vs, coff=coff, ncols=ncols: e.copy(vs[:ntok, :ncols], ps[:ntok, coff:coff + ncols]),
                      reads=[psk], writes=[vsk])
                dram_ap, dkey = dest_fn(info, ntok)
                R.dma("sp", lambda e, vs=vs, dram_ap=dram_ap, ncols=ncols: e.dma_start(out=dram_ap, in_=vs[:ntok, :ncols]),
                      reads=[vsk], writes=[dkey])
            else:
                nh = ncols // 128
                qf, qfk = qf_rot.next()
                R.act(lambda e, qf=qf, coff=coff, ncols=ncols: e.copy(qf[:ntok, :ncols], ps[:ntok, coff:coff + ncols]),
                      reads=[psk], writes=[qfk])
                qs.append([coff, nh, gi, rope_fn, dest_fn, None, None, qf, qfk])
        psA_rot.release(psrk)
        if not qs:
            return
        yield
        for q in qs:
            coff, nh, gi, rope_fn, dest_fn, _, _, qf, qfk = q
            st, stk = stat_rot.next()
            q[5], q[6] = st, stk
            if rope_fn is not None:
                for h in range(nh):
                    R.act(lambda e, st=st, h=h, qf=qf: e.activation(
                        jk[:ntok, h * 128:(h + 1) * 128], qf[:ntok, h * 128:(h + 1) * 128], AF.Square,
                        accum_out=st[:ntok, h:h + 1]), reads=[qfk], writes=["junk", stk])
                q.append(None)
            else:
                sqb, sqk = sqb_rot.next()
                R.pool(lambda e, sqb=sqb, qf=qf, nh=nh: e.tensor_tensor(
                    sqb[:ntok, :nh * 128], qf[:ntok, :nh * 128], qf[:ntok, :nh * 128], ALU.mult), reads=[qfk], writes=[sqk])
                q.append((sqb, sqk))
        yield
        for q in qs:
            coff, nh, gi, rope_fn, dest_fn, st, stk, qf, qfk, sq = q
            if sq is not None:
                sqb, sqk = sq
                R.dve(lambda e, st=st, sqb=sqb, nh=nh: e.tensor_reduce(
                    st[:ntok, 0:nh], sqb[:ntok, :nh * 128].rearrange("p (h d) -> p h d", h=nh), AX.X, ALU.add),
                    reads=[sqk], writes=[stk])
                sqb_rot.release(sqk)
        qs = [tuple(q[:9]) for q in qs]
        for (coff, nh, gi, rope_fn, dest_fn, st, stk, qf, qfk) in qs:
            rstd_stage_ms(st, stk, ntok, nh, 1.0 / 128, 0, 12)
        yield
        for (coff, nh, gi, rope_fn, dest_fn, st, stk, qf, qfk) in qs:
            rstd_stage_lnexp(st, stk, ntok, nh, 12, 8)
        yield
        outs = []
        for (coff, nh, gi, rope_fn, dest_fn, st, stk, qf, qfk) in qs:
            qb, qbk = qb_rot.next()
            rope = rope_fn(info) if rope_fn is not None else None
            if rope is None:
                dst, dkeys = qb, [(qbk, 0), (qbk, 1)]
            else:
                dst, dkeys = qf, [qfk]
            for h in range(nh):
                R.dve(lambda e, st=st, h=h, qf=qf, dst=dst, gi=gi: e.scalar_tensor_tensor(
                    dst[:ntok, h * 128:(h + 1) * 128], qf[:ntok, h * 128:(h + 1) * 128], st[:ntok, 8 + h:9 + h],
                    ghead[:ntok, gi * 128:(gi + 1) * 128], ALU.mult, ALU.mult),
                    reads=[qfk, stk, "ghead"], writes=dkeys)
            stat_rot.release(stk)
            if rope is None:
                qf_rot.release(qfk)
            outs.append((nh, dest_fn, qb, qbk, qf, qfk, rope))
        yield
        any_rope = False
        for (nh, dest_fn, qb, qbk, qf, qfk, rope) in outs:
            if rope is None:
                continue
            any_rope = True
            rp, rpk = rope
            W2 = nh * 128
            rt, rtk = rt_rot.next()
            qf3 = qf[:ntok, :W2].rearrange("p (h d) -> p h d", h=nh)
            qb3 = qb[:ntok, :W2].rearrange("p (h d) -> p h d", h=nh)
            x0 = qf3[:, :, 0::2]
            x1 = qf3[:, :, 1::2]
            cs = rp[:ntok, 0:64].unsqueeze(1).to_broadcast([ntok, nh, 64])
            sn = rp[:ntok, 64:128].unsqueeze(1).to_broadcast([ntok, nh, 64])
            t = rt[:ntok, :2 * W2].rearrange("p (a h d) -> p a h d", a=4, h=nh)
            R.op(rope_eng, lambda e, t=t, x0=x0, cs=cs: e.tensor_tensor(t[:, 0], x0, cs, ALU.mult), reads=[qfk, rpk], writes=[(rtk, 0)])
            R.op(rope_eng, lambda e, t=t, x1=x1, sn=sn: e.tensor_tensor(t[:, 1], x1, sn, ALU.mult), reads=[qfk, rpk], writes=[(rtk, 1)])
            R.op(rope_eng, lambda e, t=t, x0=x0, sn=sn: e.tensor_tensor(t[:, 2], x0, sn, ALU.mult), reads=[qfk, rpk], writes=[(rtk, 2)])
            R.op(rope_eng, lambda e, t=t, x1=x1, cs=cs: e.tensor_tensor(t[:, 3], x1, cs, ALU.mult), reads=[qfk, rpk], writes=[(rtk, 3)])
            R.op(rope_eng, lambda e, t=t, qb3=qb3: e.tensor_tensor(qb3[:, :, 0::2], t[:, 0], t[:, 1], ALU.subtract),
                   reads=[(rtk, 0), (rtk, 1)], writes=[(qbk, 0)])
            R.op(rope_eng, lambda e, t=t, qb3=qb3: e.tensor_tensor(qb3[:, :, 1::2], t[:, 2], t[:, 3], ALU.add),
                   reads=[(rtk, 2), (rtk, 3)], writes=[(qbk, 1)])
            qf_rot.release(qfk)
        if any_rope:
            yield
        for (nh, dest_fn, qb, qbk, qf, qfk, rope) in outs:
            pt, ptk = next_ps(pst_rot)
            for h in range(nh):
                R.pe(lambda e, h=h, pt=pt, qb=qb: e.transpose(pt[:, h * ntok:(h + 1) * ntok], qb[:ntok, h * 128:(h + 1) * 128],
                                                              ident[:ntok, :ntok]),
                     reads=[(qbk, 0), (qbk, 1), "ident"], writes=[ptk])
            qb_rot.release(qbk)
            sg, sgk = stg_rot.next()
            R.act(lambda e, sg=sg, pt=pt, nh=nh: e.copy(sg[:, :nh * ntok], pt[:, :nh * ntok]), reads=[ptk], writes=[sgk])
            dram_ap, dkey = dest_fn(info, ntok)
            R.dma("sp", lambda e, sg=sg, dram_ap=dram_ap, nh=nh: e.dma_start(
                out=dram_ap.rearrange("h p t -> p h t"), in_=sg[:, :nh * ntok].rearrange("p (h t) -> p h t", h=nh)),
                reads=[sgk], writes=[dkey])

    def a_tiles(chunk_kind, k):
        res = {}
        for g, d in enumerate(DILS):
            Lo, Lh = NOWN // d, 1024 // d
            Lr = Lo + Lh
            per = 1024 // d
            ntok = min(128, per)
            tl = []
            for cl in range(d):
                for mt in range(per // ntok):
                    start = cl + d * mt * ntok
                    stop = min(start + d * ntok, 1024)
                    csl = slice(start, stop, d) if d > 1 else slice(start, start + ntok)
                    if chunk_kind == "own":
                        m0 = k * per + mt * ntok
                        tl.append((csl, ntok, (cl * Lo + m0, cl * Lr + m0)))
                    else:
                        tl.append((csl, ntok, (None, cl * Lr + Lo + mt * ntok)))
            res[g] = tl
        return res

    def wcols(w, c0, n):
        return w[:, c0:c0 + n]

    nat_tiles = [(slice(i * 128, (i + 1) * 128), 128, i) for i in range(8)]

    def chunk_jobs(kind, k):
        jobs = []
        if kind == "mem":
            jobs.append(dict(wsegs=[wcols(w_mem, 0, 512)], tiles=nat_tiles[:2],
                             segs=[(0, 512, "qk", 5, None, lambda i, n: (KC_d[:, :, i * 128:(i + 1) * 128], "KC_d"))]))
            jobs.append(dict(wsegs=[wcols(w_mem, 512, 512)], tiles=nat_tiles[:2],
                             segs=[(0, 512, "v", 0, None, lambda i, n: (VC_d[i * 128:(i + 1) * 128, :], "VC_d"))]))
            return jobs
        if kind == "own":
            base = k * 1024
            rp = lambda i: (rope_own[:, k * 8 + i, :], "rope_own")
            cols = lambda i: slice(base + i * 128, base + (i + 1) * 128)
            jobs.append(dict(wsegs=[wcols(w_in, 2304, 512)], tiles=nat_tiles,
                             segs=[(0, 512, "qk", 2, rp, lambda i, n: (QB_d[0:4, :, cols(i)], "QB_d"))]))
            jobs.append(dict(wsegs=[wcols(w_in, 2816, 512)], tiles=nat_tiles,
                             segs=[(0, 256, "qk", 2, rp, lambda i, n: (QB_d[4:6, :, cols(i)], "QB_d")),
                                   (256, 256, "qk", 3, rp, lambda i, n: (KB_d[0:2, :, cols(i)], "KB_d"))]))
            jobs.append(dict(wsegs=[wcols(w_in, 3328, 512)], tiles=nat_tiles,
                             segs=[(0, 256, "v", 0, None, lambda i, n: (VB_d[cols(i), :], "VB_d")),
                                   (256, 256, "qk", 4, None, lambda i, n: (QC_d[0:2, :, cols(i)], "QC_d"))]))
            jobs.append(dict(wsegs=[wcols(w_in, 3840, 256)], tiles=nat_tiles,
                             segs=[(0, 256, "qk", 4, None, lambda i, n: (QC_d[2:4, :, cols(i)], "QC_d"))]))
            at = a_tiles("own", k)
            for g in range(3):
                jobs.append(dict(wsegs=[wcols(w_in, g * 256, 256), wcols(w_in, 768 + g * 256, 256)], tiles=at[g],
                                 segs=[(0, 256, "qk", 0, None, lambda inf, n, g=g: (QA_d[g][:, :, inf[0]:inf[0] + n], ("QA_d", g))),
                                       (256, 256, "qk", 1, None, lambda inf, n, g=g: (KA_d[g][:, :, inf[1]:inf[1] + n], ("KA_d", g)))]))
                jobs.append(dict(wsegs=[wcols(w_in, 1536 + g * 256, 256)], tiles=at[g],
                                 segs=[(0, 256, "v", 0, None, lambda inf, n, g=g: (VA_d[g][inf[1]:inf[1] + n, :], ("VA_d", g)))]))
            return jobs
        base = 2048 + k * 1024
        rp = lambda i: (rope_oth[:, k * 8 + i, :], "rope_oth")
        cols = lambda i: slice(base + i * 128, base + (i + 1) * 128)
        jobs.append(dict(wsegs=[wcols(w_in, 3072, 512)], tiles=nat_tiles,
                         segs=[(0, 256, "qk", 3, rp, lambda i, n: (KB_d[0:2, :, cols(i)], "KB_d")),
                               (256, 256, "v", 0, None, lambda i, n: (VB_d[cols(i), :], "VB_d"))]))
        if k == 0:
            at = a_tiles("halo", 0)
            for g in range(3):
                jobs.append(dict(wsegs=[wcols(w_in, 768 + g * 256, 256), wcols(w_in, 1536 + g * 256, 256)], tiles=at[g],
                                 segs=[(0, 256, "qk", 1, None, lambda inf, n, g=g: (KA_d[g][:, :, inf[1]:inf[1] + n], ("KA_d", g))),
                                       (256, 256, "v", 0, None, lambda inf, n, g=g: (VA_d[g][inf[1]:inf[1] + n, :], ("VA_d", g)))]))
        return jobs

    def s1_units(kind, k, hT, hT_key):
        us = []
        ntiles = 2 if kind == "mem" else 8
        for t in range(ntiles):
            if kind == "mem":
                src, goff = memx[t * 128:(t + 1) * 128, :], 32
            elif kind == "own":
                src, goff = x_own[k * 1024 + t * 128:k * 1024 + (t + 1) * 128, :], 0
            else:
                src, goff = x_oth[k * 1024 + t * 128:k * 1024 + (t + 1) * 128, :], 0

            def load_fn(src=src):
                xin, xk = xin_rot.next()
                R.dma("sp", lambda e: e.dma_start(out=xin, in_=src), writes=[xk])
                return xin, xk
            gen = norm_unit(load_fn, None, None, goff, hT, hT_key, t * 128, pst_rot, xs_rot, junk)
            if kind == "own" and k == 1 and t == 5 and "cut" in dbg:
                import itertools
                gen = itertools.islice(gen, dbg["cut"])
            us.append(gen)
        return us

    def nop_unit():
        return
        yield

    def store_hT_unit(k, hT, hT_key):
        for _ in range(8):
            yield
        R.dma("sp", lambda e: e.dma_start(out=hT_own_d[:, :, k * 1024:(k + 1) * 1024].rearrange("k p t -> p k t"), in_=hT),
              reads=[hT_key], writes=["hT_own_d"])
        return
        yield

    chunks = [("mem", 0), ("oth", 0), ("oth", 1), ("own", 0), ("own", 1)]
    all_jobs = []
    for ci, (kind, k) in enumerate(chunks):
        for job in chunk_jobs(kind, k):
            job["ci"] = ci
            all_jobs.append(job)
    units = []
    LBL = {}
    units += s1_units("mem", 0, hT_bufs[0], ("hT", 0))
    units.append(load_unit(all_jobs[0]))
    ji = 0
    for ci, (kind, k) in enumerate(chunks):
        hT, hT_key = hT_bufs[ci % 2], ("hT", ci % 2)
        prim = []
        while ji < len(all_jobs) and all_jobs[ji]["ci"] == ci:
            job = all_jobs[ji]
            if ji + 1 < len(all_jobs):
                prim.append(load_unit(all_jobs[ji + 1])); LBL[id(prim[-1])] = ('load', ji + 1)
            for (csl, ntok, info) in job["tiles"]:
                prim.append(proj_unit(job, hT, hT_key, csl, ntok, info)); LBL[id(prim[-1])] = ('proj', ji, str(csl), ntok, str(info))
            ji += 1
        sec = []
        if ci + 1 < len(chunks):
            nk, nkk = chunks[ci + 1]
            sec = s1_units(nk, nkk, hT_bufs[(ci + 1) % 2], ("hT", (ci + 1) % 2))
            if nk == "own":
                sec += [None] * 8 + [store_hT_unit(nkk, hT_bufs[(ci + 1) % 2], ("hT", (ci + 1) % 2))]
        merged = []
        np_, ns_ = len(prim), len(sec)
        si = 0
        for pi, u in enumerate(prim):
            merged.append(u)
            while si < ns_ and (si + 1) * np_ <= (pi + 1) * (ns_ + 1) * 0.8 + 1e-9 and si < ns_:
                if sec[si] is not None:
                    merged.append(sec[si])
                si += 1
                break
        while si < ns_:
            if sec[si] is not None:
                merged.append(sec[si])
                merged.append(nop_unit())
            si += 1
        units += merged
    if dbg.get('print'):
        for i, u in enumerate(units):
            print(i, LBL.get(id(u), 's1/store'))
    run_pipeline([u for i, u in enumerate(units[:dbg.get('max_units', len(units))]) if i not in dbg.get('skip', ())])

    R.barrier()
    if stop_after == 1:
        R.disabled = True

    cv = Carver(PERS)
    abias = cv.get([128, 24 * 128], F32)
    Qa = cv.get([128, 2, 2048], BF16)
    Ka = cv.get([128, 2, 3072], BF16)
    Va = cv.get([128, 32, 256], BF16)
    Uacc = cv.get([128, 2, 2048], F32)
    Lacc = cv.get([128, 2, 2048], F32)
    sb_rot = Rot("sb", [cv.get([128, 384], F32) for _ in range(4)])
    P_rot = Rot("P", [cv.get([128, 384], BF16) for _ in range(4)])
    rl_a = Qa.rearrange("p h n -> p (h n)").bitcast(F32)
    oa_bf = Ka[:, 0, 0:2048]

    R.dma("sp", lambda e: e.dma_start(out=abias, in_=abias_d), writes=["abias"])
    psS_rot = Rot("ps", [(i, psb[i]) for i in range(0, 4)])
    psOL_rot = Rot("ps", [(4, 5), (6, 7)])

    def a_unit(g, h, qcol, segs, asl):
        S, Sk = next_ps(psS_rot)
        for si, (kcol, nk_s, vap, bt) in enumerate(segs):
            R.pe(lambda e, si=si, kcol=kcol, nk_s=nk_s: e.matmul(
                S[0:nk_s, si * 128:(si + 1) * 128], Ka[:, h, kcol:kcol + nk_s], Qa[:, h, qcol:qcol + 128],
                start=True, stop=True), reads=["Ka", "Qa"], writes=[Sk])
        yield
        sb, sbk = sb_rot.next()
        for si, (kcol, nk_s, vap, bt) in enumerate(segs):
            R.dve(lambda e, si=si, nk_s=nk_s, bt=bt: e.scalar_tensor_tensor(
                sb[0:nk_s, si * 128:(si + 1) * 128], S[0:nk_s, si * 128:(si + 1) * 128], SCALE,
                abias[0:nk_s, bt * 128:(bt + 1) * 128], ALU.mult, ALU.add),
                reads=[Sk, "abias"], writes=[(sbk, si)])
        yield
        Pt, Pk = P_rot.next()
        for si, (kcol, nk_s, vap, bt) in enumerate(segs):
            R.act(lambda e, si=si, nk_s=nk_s: e.activation(
                Pt[0:nk_s, si * 128:(si + 1) * 128], sb[0:nk_s, si * 128:(si + 1) * 128], AF.Exp),
                reads=[(sbk, si)], writes=[(Pk, si)])
        yield
        (io, il), _ = psOL_rot.next()
        O, Ok = psb[io], ("ps", io)
        L, Lk = psb[il], ("ps", il)
        ns = len(segs)
        for si, (kcol, nk_s, vap, bt) in enumerate(segs):
            R.pe(lambda e, si=si, nk_s=nk_s, vap=vap: e.matmul(
                O[:, 0:128], vap, Pt[0:nk_s, si * 128:(si + 1) * 128], start=(si == 0), stop=(si == ns - 1)),
                reads=["Va", (Pk, si)], writes=[Ok])
        for si, (kcol, nk_s, vap, bt) in enumerate(segs):
            R.pe(lambda e, si=si, nk_s=nk_s: e.matmul(
                L[:, 0:128], ones[0:nk_s, :], Pt[0:nk_s, si * 128:(si + 1) * 128], start=(si == 0), stop=(si == ns - 1)),
                reads=["ones", (Pk, si)], writes=[Lk])
        yield
        ukey = ("Uacc", h)
        lkey = ("Lacc", h)
        if g == 0:
            R.dve(lambda e: e.tensor_copy(Uacc[:, h, asl], O[:, 0:128]), reads=[Ok], writes=[ukey])
            R.dve(lambda e: e.tensor_copy(Lacc[:, h, asl], L[:, 0:128]), reads=[Lk], writes=[lkey])
        else:
            R.dve(lambda e: e.tensor_tensor(Uacc[:, h, asl], Uacc[:, h, asl], O[:, 0:128], ALU.add), reads=[Ok, ukey], writes=[ukey])
            R.dve(lambda e: e.tensor_tensor(Lacc[:, h, asl], Lacc[:, h, asl], L[:, 0:128], ALU.add), reads=[Lk, lkey], writes=[lkey])

    for g, d in enumerate(DILS):
        Lo, Lh = NOWN // d, 1024 // d
        Lr = Lo + Lh
        R.dma("sp", lambda e, g=g: e.dma_start(out=Qa, in_=QA_d[g].rearrange("h p n -> p h n")), reads=[("QA_d", g)], writes=["Qa"])
        R.dma("sp", lambda e, g=g: e.dma_start(out=Ka, in_=KA_d[g].rearrange("h p n -> p h n")), reads=[("KA_d", g)], writes=["Ka"])
        if g < 2:
            R.dma("sp", lambda e, g=g: e.dma_start(out=Va[:, 0:24, :], in_=VA_d[g].rearrange("(t p) c -> p t c", p=128)),
                  reads=[("VA_d", g)], writes=["Va"])
        else:
            vv = VA_d[g].rearrange("(c r) x -> c r x", r=192)
            R.dma("sp", lambda e, vv=vv: e.dma_start(out=Va[:, 0:16, :], in_=vv[:, 0:128, :].rearrange("c p x -> p c x")),
                  reads=[("VA_d", g)], writes=["Va"])
            R.dma("sp", lambda e, vv=vv: e.dma_start(out=Va[0:64, 16:32, :], in_=vv[:, 128:192, :].rearrange("c p x -> p c x")),
                  reads=[("VA_d", g)], writes=["Va"])
        aunits = []
        for h in range(2):
            if g < 2:
                nq, nk = Lo // 128, Lr // 128
                bbase = (g * 2 + h) * 5
                for cl in range(d):
                    for qi in range(nq):
                        segs = []
                        for r, kt in enumerate(((qi - 1) % nk, qi, (qi + 1) % nk)):
                            bt = bbase + r
                            if r == 0 and qi == 0:
                                bt = bbase + 3
                            if r == 2 and qi == nq - 1:
                                bt = bbase + 4
                            segs.append((cl * Lr + kt * 128, 128, Va[:, cl * nk + kt, h * 128:(h + 1) * 128], bt))
                        a0 = cl + d * qi * 128
                        asl = slice(a0, a0 + 128) if d == 1 else slice(a0, min(a0 + d * 128, 2048), d)
                        aunits.append(a_unit(g, h, cl * Lo + qi * 128, segs, asl))
            else:
                bbase = 20 + h * 2
                for cl in range(16):
                    segs = [(cl * 192, 128, Va[:, cl, h * 128:(h + 1) * 128], bbase),
                            (cl * 192 + 128, 64, Va[0:64, 16 + cl, h * 128:(h + 1) * 128], bbase + 1)]
                    aunits.append(a_unit(g, h, cl * 128, segs, slice(cl, 2048, 16)))
        run_pipeline(aunits)
    for h in range(2):
        R.dve(lambda e, h=h: e.reciprocal(rl_a, Lacc[:, h, :]), reads=[("Lacc", h)], writes=["Qa"])
        R.dve(lambda e, h=h: e.tensor_tensor(oa_bf, Uacc[:, h, :], rl_a, ALU.mult), reads=[("Uacc", h), "Qa"], writes=["Ka"])
        R.dma("sp", lambda e, h=h: e.dma_start(out=OT_d[h], in_=oa_bf), reads=["Ka"], writes=["OT_d"])

    KBs = cv.get([128, 2, 4096], BF16)
    VBs = cv.get([128, 32, 256], BF16)
    QBs = cv.get([128, 6, 2048], BF16)
    KCs = cv.get([128, 4, 256], BF16)
    VCs = cv.get([128, 2, 512], BF16)
    QCs = cv.get([128, 4, 2048], BF16)
    P2_rot = Rot("P2", [cv.get([128, 512], BF16) for _ in range(4)])
    rl_rot = Rot("rl", [cv.get([128, 512], F32) for _ in range(2)])
    ob_rot = Rot("ob", [cv.get([128, 512], BF16) for _ in range(2)])

    R.dma("sp", lambda e: e.dma_start(out=KBs, in_=KB_d.rearrange("h p n -> p h n")), reads=["KB_d"], writes=["KBs"])
    R.dma("sp", lambda e: e.dma_start(out=VBs, in_=VB_d.rearrange("(t p) c -> p t c", p=128)), reads=["VB_d"], writes=["VBs"])
    R.dma("sp", lambda e: e.dma_start(out=QBs, in_=QB_d.rearrange("h p n -> p h n")), reads=["QB_d"], writes=["QBs"])
    R.dma("sp", lambda e: e.dma_start(out=KCs, in_=KC_d.rearrange("h p n -> p h n")), reads=["KC_d"], writes=["KCs"])
    R.dma("sp", lambda e: e.dma_start(out=VCs, in_=VC_d.rearrange("(t p) c -> p t c", p=128)), reads=["VC_d"], writes=["VCs"])
    R.dma("sp", lambda e: e.dma_start(out=QCs, in_=QC_d.rearrange("h p n -> p h n")), reads=["QC_d"], writes=["QCs"])

    psS2_rot = Rot("ps", [(i, psb[i]) for i in range(0, 4)])
    psOL2_rot = Rot("ps", [(4, 5), (6, 7)])

    def dense_attn(Ks, Kkey, kh, Vs, Vkey, vcol, nkt, Qs, Qkey, qh, qb, ot_idx):
        q_ap = Qs[:, qh, qb * 512:(qb + 1) * 512]
        (io, il), _ = psOL2_rot.next()
        O, Ok = psb[io], ("ps", io)
        L, Lk = psb[il], ("ps", il)
        pend = []

        def issue_s(kt):
            S, Sk = next_ps(psS2_rot)
            R.pe(lambda e, S=S, kt=kt: e.matmul(S, Ks[:, kh, kt * 128:(kt + 1) * 128], q_ap, start=True, stop=True),
                 reads=[Kkey, Qkey], writes=[Sk])
            Pt, Pk = P2_rot.next()
            pump_tick[0] += 1
            wr = [Pk]
            if nkt > 2 and pump_tick[0] % 7 == 0:
                wr.append(("pace", pump_tick[0]))
            R.act(lambda e, S=S, Pt=Pt: e.activation(Pt, S, AF.Exp, scale=SCALE), reads=[Sk], writes=wr)
            if len(wr) > 1:
                pump_conv(1, pace=[wr[1]])
            pend.append((Pt, Pk))

        issue_s(0)
        if nkt > 1:
            issue_s(1)
        for kt in range(nkt):
            if kt + 2 < nkt:
                issue_s(kt + 2)
            Pt, Pk = pend.pop(0)
            R.pe(lambda e, Pt=Pt, kt=kt: e.matmul(O, Vs[:, kt, vcol:vcol + 128], Pt, start=(kt == 0), stop=(kt == nkt - 1)),
                 reads=[Vkey, Pk], writes=[Ok])
            R.pe(lambda e, Pt=Pt, kt=kt: e.matmul(L, ones, Pt, start=(kt == 0), stop=(kt == nkt - 1)),
                 reads=["ones", Pk], writes=[Lk])
        rl, rlk = rl_rot.next()
        ob, obk = ob_rot.next()
        R.dve(lambda e: e.reciprocal(rl, L), reads=[Lk], writes=[rlk])
        R.dve(lambda e: e.tensor_tensor(ob, O, rl, ALU.mult), reads=[Ok, rlk], writes=[obk])
        R.dma("sp", lambda e: e.dma_start(out=OT_d[ot_idx, :, qb * 512:(qb + 1) * 512], in_=ob), reads=[obk], writes=["OT_d"])

    for j in range(2):
        for qb in range(4):
            for gq in range(3):
                h = 3 * j + gq
                dense_attn(KBs, "KBs", j, VBs, "VBs", j * 128, 32, QBs, "QBs", h, qb, 2 + h)
    for h in range(4):
        for qb in range(4):
            dense_attn(KCs, "KCs", h, VCs, "VCs", h * 128, 2, QCs, "QCs", h, qb, 8 + h)

    R.barrier()

    cv = Carver(PERS)
    x1 = cv.get([128, 4, D], F32)
    hTb = cv.get([128, 16, 512], BF16)
    uT = cv.get([128, 44, 512], BF16)
    OTb = cv.get([128, 12, 512], BF16)
    mT = uT[:, 12:28, :]
    wblk_rot = Rot("wblk", [cv.get([128, 16, 512], BF16) for _ in range(4)])
    xs_rot = Rot("xs", [cv.get([128, D], BF16) for _ in range(1)])
    junk = cv.get([128, D], BF16)
    sg_rot = Rot("sgm", [cv.get([128, 512], F32) for _ in range(2)])
    maccs = [cv.get([128, 512], F32) for _ in range(4)]
    tmp_rot = Rot("tmp", [cv.get([128, 512], F32) for _ in range(2)])
    ostg_rot = Rot("ostg", [cv.get([128, 512], F32) for _ in range(2)])

    psG_rot = Rot("ps", [(i, psb[i]) for i in range(0, 8)])
    pst4_rot = Rot("ps", [(6, psb[6].bitcast(BF16)), (7, psb[7].bitcast(BF16))])

    def load_w(src_blk, k, key):
        ensure_conv(key)
        wb, wk = wblk_rot.next()
        R.dma("pool", lambda e: e.dma_start(out=wb[:, 0:k, :], in_=src_blk[:, 0:k * 512].rearrange("p (k c) -> p k c", k=k)),
              reads=[(key, k0) for k0 in range(0, k, 4)], writes=[wk])
        return wb, wk

    def load_block_acts(t0):
        R.dma("sp", lambda e: e.dma_start(out=hTb, in_=hT_own_d[:, :, t0:t0 + 512].rearrange("k p t -> p k t")),
              reads=["hT_own_d"], writes=["hTb"])
        R.dma("sp", lambda e: e.dma_start(out=OTb, in_=OT_d[:, :, t0:t0 + 512].rearrange("k p t -> p k t")),
              reads=["OT_d"], writes=[("OTb", i) for i in range(12)])

    load_block_acts(0)
    for tb in range(4):
        t0 = tb * 512
        R.dma("sp", lambda e, t0=t0: e.dma_start(out=x1, in_=x_own[t0:t0 + 512, :].rearrange("(t p) c -> p t c", p=128)),
              writes=[("x1", t) for t in range(4)])
        for cq in range(4):
            for bi in range(3):
                f0, nch = br_rows[bi]
                wg, wgk = load_w(WG_d[bi * 4 + cq], 16, ("WG", bi * 4 + cq))
                wbr, wbrk = load_w(WBR_d[bi * 4 + cq], nch, ("WBR", bi * 4 + cq))
                for cc in range(4):
                    c = cq * 4 + cc
                    macc = maccs[cc]
                    mk = ("macc", cc)
                    G, Gk = next_ps(psG_rot)
                    for kc in range(16):
                        R.pe(lambda e, G=G, wg=wg, kc=kc, cc=cc: e.matmul(
                            G, wg[:, kc, cc * 128:(cc + 1) * 128], hTb[:, kc, :], start=(kc == 0), stop=(kc == 15)),
                            reads=[wgk, "hTb"], writes=[Gk])
                    Bp, Bk = next_ps(psG_rot)
                    for i in range(nch):
                        R.pe(lambda e, Bp=Bp, wbr=wbr, i=i, f0=f0, nch=nch, cc=cc: e.matmul(
                            Bp, wbr[:, i, cc * 128:(cc + 1) * 128], OTb[:, f0 + i, :], start=(i == 0), stop=(i == nch - 1)),
                            reads=[wbrk] + [("OTb", f0 + i)], writes=[Bk])
                    pump_conv(1)
                    sg, sgk = sg_rot.next()
                    R.act(lambda e, sg=sg, G=G: e.activation(sg, G, AF.Sigmoid), reads=[Gk], writes=[sgk])
                    if bi == 0:
                        R.dve(lambda e, sg=sg, Bp=Bp, macc=macc: e.tensor_tensor(macc, sg, Bp, ALU.mult), reads=[sgk, Bk], writes=[mk])
                    elif bi == 1:
                        tm, tmk = tmp_rot.next()
                        R.dve(lambda e, sg=sg, Bp=Bp, tm=tm: e.tensor_tensor(tm, sg, Bp, ALU.mult), reads=[sgk, Bk], writes=[tmk])
                        R.dve(lambda e, tm=tm, macc=macc: e.tensor_tensor(macc, macc, tm, ALU.add), reads=[tmk, mk], writes=[mk])
                    else:
                        tm, tmk = tmp_rot.next()
                        R.dve(lambda e, sg=sg, Bp=Bp, tm=tm: e.tensor_tensor(tm, sg, Bp, ALU.mult), reads=[sgk, Bk], writes=[tmk])
                        R.dve(lambda e, tm=tm, c=c, macc=macc: e.tensor_tensor(mT[:, c, :], macc, tm, ALU.add),
                              reads=[tmk, mk], writes=[("uT", 12 + c)])
        for cb in range(4):
            wb, wk = load_w(WO_d[cb], 16, ("WO", cb))
            for tt in range(4):
                ps, psk = next_ps(psG_rot)
                for c in range(16):
                    R.pe(lambda e, ps=ps, wb=wb, c=c, tt=tt: e.matmul(
                        ps, mT[:, c, tt * 128:(tt + 1) * 128], wb[:, c, :], start=(c == 0), stop=(c == 15)),
                        reads=[wk] + [("uT", 12 + c)], writes=[psk])
                R.dve(lambda e, ps=ps, tt=tt, cb=cb: e.tensor_tensor(
                    x1[:, tt, cb * 512:(cb + 1) * 512], x1[:, tt, cb * 512:(cb + 1) * 512], ps, ALU.add),
                    reads=[psk, ("x1", tt)], writes=[("x1", tt)])
        run_pipeline([norm_unit(None, x1[:, tt, :], ("x1", tt), 16, hTb, "hTb", tt * 128, pst4_rot, xs_rot, junk)
                      for tt in range(4)])
        for jg in range(11):
            wa, wak = load_w(WFI_d[2 * jg], 16, ("WFI", 2 * jg))
            wbb, wbk = load_w(WFI_d[2 * jg + 1], 16, ("WFI", 2 * jg + 1))
            for jj in range(4):
                j = jg * 4 + jj
                Pa, Pak = next_ps(psG_rot)
                for kc in range(16):
                    R.pe(lambda e, Pa=Pa, wa=wa, kc=kc, jj=jj: e.matmul(
                        Pa, wa[:, kc, jj * 128:(jj + 1) * 128], hTb[:, kc, :], start=(kc == 0), stop=(kc == 15)),
                        reads=[wak, "hTb"], writes=[Pak])
                Pb, Pbk = next_ps(psG_rot)
                for kc in range(16):
                    R.pe(lambda e, Pb=Pb, wbb=wbb, kc=kc, jj=jj: e.matmul(
                        Pb, wbb[:, kc, jj * 128:(jj + 1) * 128], hTb[:, kc, :], start=(kc == 0), stop=(kc == 15)),
                        reads=[wbk, "hTb"], writes=[Pbk])
                sg, sgk = sg_rot.next()
                R.act(lambda e, sg=sg, Pa=Pa: e.activation(sg, Pa, AF.Silu), reads=[Pak], writes=[sgk])
                R.dve(lambda e, sg=sg, Pb=Pb, j=j: e.tensor_tensor(uT[:, j, :], sg, Pb, ALU.mult),
                      reads=[sgk, Pbk], writes=[("uT", j)])
        if tb + 1 < 4:
            load_block_acts(t0 + 512)
        for cb in range(4):
            accs = [next_ps(psG_rot) for _ in range(4)]
            for kp in range(4):
                wb, wk = load_w(WFO_d[cb * 4 + kp], 11, ("WFO", cb * 4 + kp))
                for tt in range(4):
                    ps, psk = accs[tt]
                    for i in range(11):
                        j = kp * 11 + i
                        R.pe(lambda e, ps=ps, wb=wb, i=i, j=j, tt=tt, kp=kp: e.matmul(
                            ps, uT[:, j, tt * 128:(tt + 1) * 128], wb[:, i, :], start=(kp == 0 and i == 0), stop=(kp == 3 and i == 10)),
                            reads=[wk, ("uT", j)], writes=[psk])
            for tt in range(4):
                ps, psk = accs[tt]
                og, ogk = ostg_rot.next()
                R.dve(lambda e, ps=ps, og=og, tt=tt, cb=cb: e.tensor_tensor(og, x1[:, tt, cb * 512:(cb + 1) * 512], ps, ALU.add),
                      reads=[psk, ("x1", tt)], writes=[ogk])
                R.dma("sp", lambda e, og=og, tt=tt, cb=cb, t0=t0: e.dma_start(
                    out=out_d[t0 + tt * 128:t0 + (tt + 1) * 128, cb * 512:(cb + 1) * 512], in_=og), reads=[ogk], writes=["out"])

    with contextlib.ExitStack() as es:
        sems = {n: es.enter_context(nc.semaphore("s_" + n)) for n in ("pe", "act", "dve", "pool")}
        sems["dma"] = {"sp": [es.enter_context(nc.semaphore(f"dsp{i}")) for i in range(24)],
                       "pool": [es.enter_context(nc.semaphore(f"dpl{i}")) for i in range(12)],
                       "pool:cv": [es.enter_context(nc.semaphore(f"dcv{i}")) for i in range(6)]}
        block = es.enter_context(nc.Block())
        cc = R.prepare(sems)

        @block.tensor
        def _(e):
            R.run_engine("pe", e, sems)

        @block.scalar
        def _(e):
            R.run_engine("act", e, sems)

        @block.vector
        def _(e):
            R.run_engine("dve", e, sems)

        @block.gpsimd
        def _(e):
            R.run_engine("pool", e, sems)
            R.final_waits("pool", e, sems)

        @block.sync
        def _(e):
            R.run_engine("sp", e, sems)
            R.final_waits("sp", e, sems)
    build_program.R = R
    return nc, len(R.ops), cc


def _t5_bucket(rel):
    nb = 16
    ret = np.where(rel > 0, nb, 0)
    n = np.abs(rel)
    max_exact = 8
    large = max_exact + (np.log(np.maximum(n, 1).astype(np.float32) / max_exact)
                         / math.log(1024 / max_exact) * (nb - max_exact)).astype(np.int32)
    large = np.minimum(large, nb - 1)
    return ret + np.where(n < max_exact, n, large)


def _rope_table(pos):
    r = (pos // 64).astype(np.float32)
    c = (pos % 64).astype(np.float32)
    inv = (10000.0 ** (-np.arange(32, dtype=np.float32) / 32)).astype(np.float32)
    ang = np.concatenate([r[:, None] * inv, c[:, None] * inv], axis=-1).astype(np.float32)
    return np.concatenate([np.cos(ang), np.sin(ang)], axis=-1).astype(np.float32)


def _bias_tiles(rel_bias, parity):
    tiles = np.full((24, 128, 128), NEGB, dtype=np.float32)
    kk = np.arange(128)[:, None]
    qq = np.arange(128)[None, :]

    def tile(rel, d, head, kvalid=None):
        valid = np.abs(rel) <= 64
        if kvalid is not None:
            valid = valid & kvalid
        b = _t5_bucket(rel * d)
        return np.where(valid, rel_bias[b, head], np.float32(NEGB)).astype(np.float32)

    for g, d in enumerate((1, 4)):
        for h in range(2):
            head = 2 * g + h
            base = (g * 2 + h) * 5
            tiles[base + 0] = tile(kk - 128 - qq, d, head)
            tiles[base + 1] = tile(kk - qq, d, head)
            tiles[base + 2] = tile(kk + 128 - qq, d, head)
            if parity == 1:
                tiles[base + 3] = tiles[base + 0]
            if parity == 0:
                tiles[base + 4] = tiles[base + 2]
    for h in range(2):
        head = 4 + h
        base = 20 + h * 2
        tiles[base + 0] = tile(kk - qq, 16, head)
        k64 = (kk < 64)
        if parity == 0:
            rel = (128 + kk) - qq
        else:
            rel = (kk - 64) - qq
        t = tile(rel, 16, head, kvalid=k64)
        tiles[base + 1] = t
    return np.ascontiguousarray(tiles.transpose(1, 0, 2).reshape(128, 24 * 128))


_PROG = None


def kernel(x, mem, rel_bias, g_mix, w_in, g_qa, g_ka, g_qb, g_kb, g_mem, w_mem_kv, g_qc, g_kc,
           w_br_a, w_br_b, w_br_c, w_o, g_ffn, w_ffn_in, w_ffn_out):
    global _PROG
    if _PROG is None:
        _PROG = build_program()
    nc = _PROG[0]
    in_maps = make_in_maps(x, mem, rel_bias, g_mix, w_in, g_qa, g_ka, g_qb, g_kb, g_mem, w_mem_kv, g_qc, g_kc,
                           w_br_a, w_br_b, w_br_c, w_o, g_ffn, w_ffn_in, w_ffn_out)
    res = run_bass_kernel_spmd(nc, in_maps, core_ids=list(range(8)))
    out = np.empty((4, SEQ, D), dtype=np.float32)
    for core in range(8):
        b, par = core // 2, core % 2
        out[b, par * 2048:(par + 1) * 2048] = res.results[core]["out"]
    return out


def make_in_maps(x, mem, rel_bias, g_mix, w_in, g_qa, g_ka, g_qb, g_kb, g_mem, w_mem_kv, g_qc, g_kc,
                 w_br_a, w_br_b, w_br_c, w_o, g_ffn, w_ffn_in, w_ffn_out):
    f = lambda a: np.ascontiguousarray(np.asarray(a, dtype=np.float32))
    x = f(x)
    mem = f(mem)
    rel_bias = f(rel_bias)
    col = lambda g: f(g)[0].reshape(16, 128).T
    gcol = np.ascontiguousarray(np.concatenate([col(g_mix), col(g_ffn), col(g_mem)], axis=1))
    ghead = np.ascontiguousarray(np.broadcast_to(
        np.concatenate([f(g)[0] for g in (g_qa, g_ka, g_qb, g_kb, g_qc, g_kc)])[None, :], (128, 768)))
    ident = np.eye(128, dtype=np.float32)
    shared = {
        "w_in": f(w_in)[0], "w_mem": f(w_mem_kv)[0], "w_br_a": f(w_br_a)[0], "w_br_b": f(w_br_b)[0],
        "w_br_c": f(w_br_c)[0], "w_o": f(w_o)[0], "w_fi": f(w_ffn_in)[0], "w_fo": f(w_ffn_out)[0],
        "gcol": gcol, "ghead": ghead, "ident": ident,
    }
    in_maps = []
    for core in range(8):
        b, par = core // 2, core % 2
        if par == 0:
            own = x[b, 0:2048]
            oth = x[b, 2048:4096]
            pos_own = np.arange(0, 2048)
            pos_oth = np.arange(2048, 4096)
        else:
            own = x[b, 2048:4096]
            oth = np.concatenate([x[b, 1024:2048], x[b, 0:1024]], axis=0)
            pos_own = np.arange(2048, 4096)
            pos_oth = np.concatenate([np.arange(1024, 2048), np.arange(0, 1024)])
        m = dict(shared)
        m["x_own"] = np.ascontiguousarray(own)
        m["x_oth"] = np.ascontiguousarray(oth)
        m["memx"] = np.ascontiguousarray(mem[b])
        m["rope_own"] = np.ascontiguousarray(_rope_table(pos_own).reshape(16, 128, 128).transpose(1, 0, 2))
        m["rope_oth"] = np.ascontiguousarray(_rope_table(pos_oth).reshape(16, 128, 128).transpose(1, 0, 2))
        m["abias"] = _bias_tiles(rel_bias, par)
        in_maps.append(m)
    return in_maps
```
